# Optimizing a Trainium2 kernel written in Bass

```python
import jax
import jax.numpy as jnp
from jax import lax
import numpy as np

D_MODEL = 1024
BATCH = 4
SEQ = 8192
DEPTH = 2

GRID_W = 64
CTX_LEN = 256
D_FF = 4 * D_MODEL
NORM_EPS = 1e-6
ROPE_THETA = 10000.0
Q_BLOCK = 128
N_MOD = 6

MLA_HEADS = 8
MLA_Q_RANK = 256
MLA_KV_RANK = 128
MLA_NOPE = 64
MLA_ROPE = 32
MLA_V = 64

ML_HEADS = 4
ML_DK = 128
ML_DV = 128
ML_CONV = 3
ML_CHUNK = 128
ML_F_BIAS_LO = 3.0
ML_F_BIAS_HI = 6.0

GQA_HEADS = 8
GQA_KV_HEADS = 2
GQA_GROUP = GQA_HEADS // GQA_KV_HEADS
GQA_DH = 128

N_GATES = 4 * ML_HEADS
EVEN_WIDTHS = (MLA_Q_RANK, MLA_KV_RANK, MLA_ROPE, 2 * ML_HEADS * ML_DK, ML_HEADS * ML_DV, ML_HEADS * ML_DV, N_GATES)
EVEN_SPLITS = tuple(int(s) for s in np.cumsum(EVEN_WIDTHS)[:-1])
EVEN_IN = sum(EVEN_WIDTHS)
EVEN_MIX = MLA_HEADS * MLA_V + ML_HEADS * ML_DV
ODD_SPLITS = (GQA_HEADS * GQA_DH, (GQA_HEADS + GQA_KV_HEADS) * GQA_DH)
ODD_IN = (GQA_HEADS + 2 * GQA_KV_HEADS) * GQA_DH
ODD_MIX = GQA_HEADS * GQA_DH

kernel_name = 'hybrid_mla_mlstm_gqa_dit_block'


def rms_norm(x, g):
    xf = x.astype(jnp.float32)
    y = xf * lax.rsqrt(jnp.mean(xf * xf, axis=-1, keepdims=True) + NORM_EPS)
    return (y * g.astype(jnp.float32)).astype(x.dtype)


def modulate(x, g, shift, scale):
    return rms_norm(x, g) * (1 + scale[:, None, :]) + shift[:, None, :]


def adaln(cvec, ada_w, ada_b):
    return jnp.split(jax.nn.silu(cvec) @ ada_w + ada_b, N_MOD, axis=-1)


def grid_angles(n_tokens, d_rot):
    rows = n_tokens // GRID_W
    row, col = jnp.meshgrid(jnp.arange(rows), jnp.arange(GRID_W), indexing='ij')
    pos = jnp.stack([row.reshape(-1), col.reshape(-1)], axis=-1).astype(jnp.float32)
    n_freq = d_rot // 4
    freqs = ROPE_THETA ** (-jnp.arange(n_freq, dtype=jnp.float32) / n_freq)
    return pos[:, :, None] * freqs


def apply_rope(x, ang):
    xr = x.reshape(x.shape[:-1] + (2, 2, ang.shape[-1]))
    x0, x1 = xr[..., 0, :], xr[..., 1, :]
    cos = jnp.cos(ang).astype(x.dtype)
    sin = jnp.sin(ang).astype(x.dtype)
    out = jnp.stack([x0 * cos - x1 * sin, x1 * cos + x0 * sin], axis=-2)
    return out.reshape(x.shape)


def block_attention(q, k, v, scale):
    B, Hk, G, T, dq = q.shape
    nb = T // Q_BLOCK
    qb = jnp.moveaxis(q.reshape(B, Hk, G, nb, Q_BLOCK, dq), 3, 0)

    def one_block(qblk):
        s = jnp.einsum('bhgqd,bhkd->bhgqk', qblk, k, preferred_element_type=jnp.float32) * scale
        p = jax.nn.softmax(s, axis=-1)
        return jnp.einsum('bhgqk,bhkd->bhgqd', p.astype(v.dtype), v)

    out = lax.map(one_block, qb)
    return jnp.moveaxis(out, 0, 3).reshape(B, Hk, G, T, v.shape[-1])


def merge_heads(o):
    B, T = o.shape[0], o.shape[3]
    return o.transpose(0, 3, 1, 2, 4).reshape(B, T, -1)


def short_conv(x, w):
    pad = w.shape[0] // 2
    return lax.conv_general_dilated(x, w[:, None, :].astype(x.dtype), window_strides=(1,), padding=[(pad, pad)],
                                    dimension_numbers=('NWC', 'WIO', 'NWC'), feature_group_count=x.shape[-1])


def mla_q(cq_pre, q_norm, w_uq, ang):
    B, T = cq_pre.shape[:2]
    q = (rms_norm(cq_pre, q_norm) @ w_uq).reshape(B, T, MLA_HEADS, MLA_NOPE + MLA_ROPE).transpose(0, 2, 1, 3)
    q_pe = q[..., MLA_NOPE:]
    if ang is not None:
        q_pe = apply_rope(q_pe, ang)
    return jnp.concatenate([q[..., :MLA_NOPE], q_pe], axis=-1)[:, :, None]


def mla_kv(ckv_pre, k_pe, kv_norm, w_ukv, ang):
    B, T = ckv_pre.shape[:2]
    kv = (rms_norm(ckv_pre, kv_norm) @ w_ukv).reshape(B, T, MLA_HEADS, MLA_NOPE + MLA_V).transpose(0, 2, 1, 3)
    k_pe = k_pe[:, None]
    if ang is not None:
        k_pe = apply_rope(k_pe, ang)
    k_pe = jnp.broadcast_to(k_pe, (B, MLA_HEADS, T, MLA_ROPE))
    k = jnp.concatenate([kv[..., :MLA_NOPE], k_pe], axis=-1)
    return k, kv[..., MLA_NOPE:]


def mla_mixer(px, pc, q_norm, w_uq, kv_norm, w_ukv, ang, with_ctx):
    scale = (MLA_NOPE + MLA_ROPE) ** -0.5
    kc, vc = mla_kv(pc[1], pc[2], kv_norm, w_ukv, None)
    kx, vx = mla_kv(px[1], px[2], kv_norm, w_ukv, ang)
    qx = mla_q(px[0], q_norm, w_uq, ang)
    out_x = merge_heads(block_attention(qx, jnp.concatenate([kc, kx], axis=2), jnp.concatenate([vc, vx], axis=2), scale))
    out_c = merge_heads(block_attention(mla_q(pc[0], q_norm, w_uq, None), kc, vc, scale)) if with_ctx else None
    return out_x, out_c


def mlstm_scan(q, k, v, log_i, log_f, state):
    B, H, T, dv = v.shape
    nc = T // ML_CHUNK

    def chunks(a):
        return jnp.moveaxis(a.reshape((B, H, nc, ML_CHUNK) + a.shape[3:]), 2, 0)

    lower = jnp.tril(jnp.ones((ML_CHUNK, ML_CHUNK), dtype=bool))

    def step(carry, xs):
        C, n, m = carry
        qc, kc, vc, li, lf = xs
        b = jnp.cumsum(lf, axis=-1)
        logw = jnp.where(lower, b[..., :, None] - b[..., None, :] + li[..., None, :], -jnp.inf)
        m_t = jnp.maximum(b + m[..., None], jnp.max(logw, axis=-1))
        w_state = jnp.exp(b + m[..., None] - m_t)
        s = jnp.einsum('bhtd,bhsd->bhts', qc, kc) * jnp.exp(logw - m_t[..., None])
        num = w_state[..., None] * jnp.einsum('bhtd,bhde->bhte', qc, C) + jnp.einsum('bhts,bhse->bhte', s, vc)
        den = w_state * jnp.einsum('bhtd,bhd->bht', qc, n) + jnp.sum(s, axis=-1)
        h = num / jnp.maximum(jnp.abs(den), jnp.exp(-m_t))[..., None]
        b_end = b[..., -1]
        g = b_end[..., None] - b + li
        m_new = jnp.maximum(b_end + m, jnp.max(g, axis=-1))
        decay = jnp.exp(b_end + m - m_new)
        wk = jnp.exp(g - m_new[..., None])[..., None] * kc
        C_new = decay[..., None, None] * C + jnp.einsum('bhsd,bhse->bhde', wk, vc)
        n_new = decay[..., None] * n + jnp.sum(wk, axis=2)
        return (C_new, n_new, m_new), h

    state, hs = lax.scan(step, state, (chunks(q), chunks(k), chunks(v), chunks(log_i), chunks(log_f)))
    return jnp.moveaxis(hs, 0, 2).reshape(B, H, T, dv), state


def mlstm_prep(p, conv_w, gate_b):
    qk_pre, v, o_pre, g = p
    B, T = v.shape[:2]
    qk = jax.nn.silu(short_conv(qk_pre, conv_w))
    q, k = jnp.split(qk, 2, axis=-1)

    def heads(a):
        return a.reshape(B, T, ML_HEADS, -1).transpose(0, 2, 1, 3).astype(jnp.float32)

    g = (g + gate_b).astype(jnp.float32).reshape(B, T, 2, 2, ML_HEADS).transpose(2, 3, 0, 4, 1)
    log_i = g[:, 0]
    log_f = jax.nn.log_sigmoid(g[:, 1])
    return heads(q) * ML_DK ** -0.5, heads(k), heads(v), o_pre, log_i, log_f


def mlstm_mixer(px, pc, conv_w, gate_b, head_norm, with_ctx):
    qx, kx, vx, ox, lix, lfx = mlstm_prep(px, conv_w, gate_b)
    qc, kc, vc, oc, lic, lfc = mlstm_prep(pc, conv_w, gate_b)
    B = qx.shape[0]
    zero = (jnp.zeros((B, ML_HEADS, ML_DK, ML_DV), jnp.float32), jnp.zeros((B, ML_HEADS, ML_DK), jnp.float32),
            jnp.zeros((B, ML_HEADS), jnp.float32))

    def rev(a):
        return jnp.flip(a, axis=2)

    hc_f, st_f = mlstm_scan(qc, kc, vc, lic[0], lfc[0], zero)
    hx_f, _ = mlstm_scan(qx, kx, vx, lix[0], lfx[0], st_f)
    hc_b, st_b = mlstm_scan(rev(qc), rev(kc), rev(vc), rev(lic[1]), rev(lfc[1]), zero)
    hx_b, _ = mlstm_scan(rev(qx), rev(kx), rev(vx), rev(lix[1]), rev(lfx[1]), st_b)

    def readout(hf, hb, o_pre):
        h = rms_norm(hf + rev(hb), head_norm.reshape(ML_HEADS, 1, ML_DV))
        return (jax.nn.sigmoid(o_pre) * merge_heads(h[:, :, None])).astype(o_pre.dtype)

    out_x = readout(hx_f, hx_b, ox)
    out_c = readout(hc_f, hc_b, oc) if with_ctx else None
    return out_x, out_c


def gqa_q(pq, q_norm, ang):
    B, T = pq.shape[:2]
    q = rms_norm(pq.reshape(B, T, GQA_KV_HEADS, GQA_GROUP, GQA_DH).transpose(0, 2, 3, 1, 4), q_norm)
    return q if ang is None else apply_rope(q, ang)


def gqa_kv(pk, pv, k_norm, ang):
    B, T = pk.shape[:2]

    def heads(a):
        return a.reshape(B, T, GQA_KV_HEADS, GQA_DH).transpose(0, 2, 1, 3)

    k = rms_norm(heads(pk), k_norm)
    if ang is not None:
        k = apply_rope(k, ang)
    return k, heads(pv)


def gqa_mixer(hx, hc, q_norm, k_norm, ang, with_ctx):
    scale = GQA_DH ** -0.5
    qx_pre, kx_pre, vx_pre = jnp.split(hx, ODD_SPLITS, axis=-1)
    qc_pre, kc_pre, vc_pre = jnp.split(hc, ODD_SPLITS, axis=-1)
    kc, vc = gqa_kv(kc_pre, vc_pre, k_norm, None)
    kx, vx = gqa_kv(kx_pre, vx_pre, k_norm, ang)
    qx = gqa_q(qx_pre, q_norm, ang)
    out_x = merge_heads(block_attention(qx, jnp.concatenate([kc, kx], axis=2), jnp.concatenate([vc, vx], axis=2), scale))
    out_c = merge_heads(block_attention(gqa_q(qc_pre, q_norm, None), kc, vc, scale)) if with_ctx else None
    return out_x, out_c


def finish_sublayers(s, mix, mods, w_out, norm2, w1, w2):
    s = s + mods[2][:, None] * (mix @ w_out)
    h = modulate(s, norm2, mods[3], mods[4])
    return s + mods[5][:, None] * (jnp.square(jax.nn.relu(h @ w1)) @ w2)


def even_layer(x, ctx, c, c_ctx, prm, ang, update_ctx):
    (ada_w, ada_b, norm1, w_in, mla_q_norm, mla_w_uq, mla_kv_norm, mla_w_ukv,
     ml_conv, ml_gate_b, ml_head_norm, w_out, norm2, w1, w2) = prm
    mx = adaln(c, ada_w, ada_b)
    mc = adaln(c_ctx[None], ada_w, ada_b)
    px = jnp.split(modulate(x, norm1, mx[0], mx[1]) @ w_in, EVEN_SPLITS, axis=-1)
    pc = jnp.split(modulate(ctx, norm1, mc[0], mc[1]) @ w_in, EVEN_SPLITS, axis=-1)
    a_x, a_c = mla_mixer(px[:3], pc[:3], mla_q_norm, mla_w_uq, mla_kv_norm, mla_w_ukv, ang, update_ctx)
    b_x, b_c = mlstm_mixer(px[3:], pc[3:], ml_conv, ml_gate_b, ml_head_norm, update_ctx)
    x = finish_sublayers(x, jnp.concatenate([a_x, b_x], axis=-1), mx, w_out, norm2, w1, w2)
    if update_ctx:
        ctx = finish_sublayers(ctx, jnp.concatenate([a_c, b_c], axis=-1), mc, w_out, norm2, w1, w2)
    return x, ctx


def odd_layer(x, ctx, c, c_ctx, prm, ang, update_ctx):
    (ada_w, ada_b, norm1, w_in, q_norm, k_norm, w_out, norm2, w1, w2) = prm
    mx = adaln(c, ada_w, ada_b)
    mc = adaln(c_ctx[None], ada_w, ada_b)
    hx = modulate(x, norm1, mx[0], mx[1]) @ w_in
    hc = modulate(ctx, norm1, mc[0], mc[1]) @ w_in
    att_x, att_c = gqa_mixer(hx, hc, q_norm, k_norm, ang, update_ctx)
    x = finish_sublayers(x, att_x, mx, w_out, norm2, w1, w2)
    if update_ctx:
        ctx = finish_sublayers(ctx, att_c, mc, w_out, norm2, w1, w2)
    return x, ctx


def setup_inputs(seed: int = 0) -> dict:
    key = jax.random.key(seed)
    ks = iter(jax.random.split(key, 40))

    def nrm(shape, s):
        return jax.random.normal(next(ks), shape, jnp.float32) * s

    def gain(n):
        return 1.0 + nrm((n,), 0.01)

    D = D_MODEL
    f_bias = jnp.linspace(ML_F_BIAS_LO, ML_F_BIAS_HI, ML_HEADS, dtype=jnp.float32)
    gate_b = nrm((2, 2, ML_HEADS), 0.01).at[:, 1].add(f_bias).reshape(-1)
    return {
        'x': nrm((BATCH, SEQ, D), 1.0),
        'c': nrm((BATCH, D), 1.0),
        'ctx': nrm((BATCH, CTX_LEN, D), 1.0),
        'c_ctx': nrm((D,), 1.0),
        'l0_ada_w': nrm((D, N_MOD * D), 0.5 * D ** -0.5),
        'l0_ada_b': nrm((N_MOD * D,), 0.01),
        'l0_norm1': gain(D),
        'l0_w_in': nrm((D, EVEN_IN), D ** -0.5),
        'l0_mla_q_norm': gain(MLA_Q_RANK),
        'l0_mla_w_uq': nrm((MLA_Q_RANK, MLA_HEADS * (MLA_NOPE + MLA_ROPE)), MLA_Q_RANK ** -0.5),
        'l0_mla_kv_norm': gain(MLA_KV_RANK),
        'l0_mla_w_ukv': nrm((MLA_KV_RANK, MLA_HEADS * (MLA_NOPE + MLA_V)), MLA_KV_RANK ** -0.5),
        'l0_ml_conv': nrm((ML_CONV, 2 * ML_HEADS * ML_DK), ML_CONV ** -0.5),
        'l0_ml_gate_b': gate_b,
        'l0_ml_head_norm': gain(ML_HEADS * ML_DV),
        'l0_w_out': nrm((EVEN_MIX, D), EVEN_MIX ** -0.5),
        'l0_norm2': gain(D),
        'l0_w1': nrm((D, D_FF), D ** -0.5),
        'l0_w2': nrm((D_FF, D), D_FF ** -0.5),
        'l1_ada_w': nrm((D, N_MOD * D), 0.5 * D ** -0.5),
        'l1_ada_b': nrm((N_MOD * D,), 0.01),
        'l1_norm1': gain(D),
        'l1_w_in': nrm((D, ODD_IN), D ** -0.5),
        'l1_q_norm': gain(GQA_DH),
        'l1_k_norm': gain(GQA_DH),
        'l1_w_out': nrm((ODD_MIX, D), ODD_MIX ** -0.5),
        'l1_norm2': gain(D),
        'l1_w1': nrm((D, D_FF), D ** -0.5),
        'l1_w2': nrm((D_FF, D), D_FF ** -0.5),
        'final_norm': gain(D),
    }


def reference(x, c, ctx, c_ctx,
              l0_ada_w, l0_ada_b, l0_norm1, l0_w_in, l0_mla_q_norm, l0_mla_w_uq, l0_mla_kv_norm, l0_mla_w_ukv,
              l0_ml_conv, l0_ml_gate_b, l0_ml_head_norm, l0_w_out, l0_norm2, l0_w1, l0_w2,
              l1_ada_w, l1_ada_b, l1_norm1, l1_w_in, l1_q_norm, l1_k_norm, l1_w_out, l1_norm2, l1_w1, l1_w2,
              final_norm):
    T = x.shape[1]
    ang_mla = grid_angles(T, MLA_ROPE)
    ang_gqa = grid_angles(T, GQA_DH)
    layers = (
        (even_layer, (l0_ada_w, l0_ada_b, l0_norm1, l0_w_in, l0_mla_q_norm, l0_mla_w_uq, l0_mla_kv_norm, l0_mla_w_ukv,
                      l0_ml_conv, l0_ml_gate_b, l0_ml_head_norm, l0_w_out, l0_norm2, l0_w1, l0_w2), ang_mla),
        (odd_layer, (l1_ada_w, l1_ada_b, l1_norm1, l1_w_in, l1_q_norm, l1_k_norm, l1_w_out, l1_norm2, l1_w1, l1_w2), ang_gqa),
    )
    for i in range(DEPTH):
        layer_fn, prm, ang = layers[i]
        x, ctx = layer_fn(x, ctx, c, c_ctx, prm, ang, i < DEPTH - 1)
    return rms_norm(x, final_norm)
```

```python
import numpy as np
from contextlib import ExitStack
import concourse.bass as bass
import concourse.mybir as mybir
from concourse.bass_utils import run_bass_kernel_spmd

F32 = mybir.dt.float32
BF16 = mybir.dt.bfloat16
AF = mybir.ActivationFunctionType
ALU = mybir.AluOpType
AX = mybir.AxisListType

D = 1024
T = 8192
CT = 256
R = T + CT
NCORES = 8
TH = T // 2
EPS = 1e-6
EPOCH = 30000
HOIST = True
STRICT_SAME_ENGINE = True
DVE_L = True
EPI_DEFER = 4
ENGS = ('pe', 'act', 'dve', 'pool', 'sp')
BNAME = {'pe': 'tensor', 'act': 'scalar', 'dve': 'vector', 'pool': 'gpsimd', 'sp': 'sync'}


class Tl:
    def __init__(s, t, name):
        s.t = t; s.name = name
        s.w = None
        s.r = {}
        s.dsem = None; s.dent = None; s.dgen = -1

    def __getitem__(s, k):
        return s.t[k]


class Prog:
    def __init__(s, nc, st):
        s.nc = nc; s.st = st
        s.ops = {e: [] for e in ENGS}
        s.esem = {}; s.ecnt = {e: 0 for e in ENGS}
        s.waited = {e: {} for e in ENGS}
        s.sems = {}
        s.nsem = 0
        s.last = {}
        s.gen = 0; s.gseq = 0
        s.keys = {e: [] for e in ENGS}; s.floor = {e: 0 for e in ENGS}
        s.dpool = []; s.dfree = []

    def newsem(s, name):
        s.nsem += 1
        h = s.st.enter_context(s.nc.semaphore(f"{name}_{s.nsem}"))
        return h

    def sb(s, st, name, shape, dt):
        t = Tl(st.enter_context(s.nc.sbuf_tensor(name, list(shape), dt)), name)
        t.shape = list(shape)
        return t

    def ps(s, st, name, shape, dt):
        t = Tl(st.enter_context(s.nc.psum_tensor(name, list(shape), dt)), name)
        return t

    def _deps(s, eng, reads, writes, is_dma, dedupe=True):
        deps = []
        for t in reads:
            if t.w is not None:
                deps.append((t.w, 'raw'))
        for t in writes:
            if t.w is not None:
                deps.append((t.w, 'waw'))
            for ev in t.r.values():
                deps.append((ev, 'war'))
        out = {}
        gmax = -1
        for (sem, val, oeng, gen, gs), kind in deps:
            if gen < s.gen:
                continue
            if (not is_dma) and oeng == eng:
                if eng == 'pe' or (kind != 'raw' and not STRICT_SAME_ENGINE):
                    continue
            gmax = max(gmax, gs)
            k = id(sem)
            if dedupe and s.waited[eng].get(k, 0) >= val:
                continue
            if k not in out or out[k][1] < val:
                out[k] = (sem, val)
        if dedupe:
            for k, (sem, val) in out.items():
                s.waited[eng][k] = val
        return list(out.values()), gmax

    def _mark(s, ev, reads, writes):
        for t in writes:
            t.w = ev; t.r = {}
        for t in reads:
            k = id(ev[0])
            t.r[k] = ev
        s.last[id(ev[0])] = (ev[0], ev[1])

    def _append(s, eng, rec, key):
        s.ops[eng].append(rec); s.keys[eng].append(key)

    def op(s, eng, fn, r=(), w=()):
        waits, _ = s._deps(eng, r, w, False)
        c = s.ecnt[eng]
        ep = c // EPOCH
        key = (eng, ep)
        if key not in s.esem:
            s.esem[key] = s.newsem(f"e{eng}{ep}")
        sem = s.esem[key]
        val = c % EPOCH + 1
        s.ecnt[eng] = c + 1
        s.gseq += 1
        ev = (sem, val, eng, s.gen, s.gseq)
        s._append(eng, (waits, fn, (sem, 1)), s.gseq)
        s._mark(ev, r, w)

    def pe(s, fn, r=(), w=()): s.op('pe', fn, r, w)
    def act(s, fn, r=(), w=()): s.op('act', fn, r, w)
    def dve(s, fn, r=(), w=()): s.op('dve', fn, r, w)
    def pool(s, fn, r=(), w=()): s.op('pool', fn, r, w)

    def _grab_dsem(s, tile):
        while s.dfree:
            ent = s.dfree.pop()
            if ent[1] + 4096 < EPOCH:
                break
        else:
            ent = [s.newsem("dq"), 0]
            s.dpool.append(ent)
        tile.dent = ent
        tile.dsem = ent[0]

    def dma(s, q, out, in_, tile, load, extra_r=(), extra_w=(), hoist=True, **kw):
        r = list(extra_r); w = list(extra_w)
        if load:
            w.append(tile)
        else:
            r.append(tile)
        hoist = hoist and load and HOIST
        waits, gmax = s._deps(q, r, w, True, dedupe=not hoist)
        if tile.dsem is None or tile.dgen != s.gen or tile.dent[1] + 16 > EPOCH:
            s._grab_dsem(tile); tile.dgen = s.gen
        tile.dent[1] += 16
        s.gseq += 1
        ev = (tile.dsem, tile.dent[1], None, s.gen, s.gseq)
        rec = (waits, (lambda e, o=out, i=in_, k=kw: e.dma_start(out=o, in_=i, **k)), (tile.dsem, 16))
        if hoist:
            import bisect
            key = gmax + 0.5
            pos = max(bisect.bisect_right(s.keys[q], key), s.floor[q])
            s.ops[q].insert(pos, rec); s.keys[q].insert(pos, key)
            ev = (tile.dsem, tile.dent[1], None, s.gen, key)
        else:
            s._append(q, rec, s.gseq)
        s._mark(ev, r, w)

    def barrier(s):
        allev = list(s.last.values())
        s.gseq += 1
        for eng in ENGS:
            waits = []
            for sem, val in allev:
                k = id(sem)
                if s.waited[eng].get(k, 0) >= val:
                    continue
                s.waited[eng][k] = val
                waits.append((sem, val))
            if waits:
                s._append(eng, (waits, None, None), s.gseq)
            s.floor[eng] = len(s.ops[eng])

    def flush(s):
        with s.nc.Block() as block:
            for eng in ENGS:
                ops = s.ops[eng]
                if not ops:
                    continue

                def body(e, ops=ops):
                    for waits, fn, inc in ops:
                        for sem, val in waits:
                            e.wait_ge(sem, val)
                        if fn is not None:
                            ins = fn(e)
                            if inc is not None:
                                ins.then_inc(inc[0], inc[1])
                getattr(block, BNAME[eng])(body)
        s.ops = {e: [] for e in ENGS}
        s.keys = {e: [] for e in ENGS}; s.floor = {e: 0 for e in ENGS}
        s.gen += 1
        s.dfree = list(s.dpool)
        s.last = {}


def w_in_names():
    return None


class K:
    pass


def build(dbg=None, stop_after=None, limit_mt=None, limit_heads=None, limit_qb=None, skip=None):
    nc = bass.Bass("TRN2", target_bir_lowering=False)
    dbg = dbg or []

    def din(name, shape, dt=F32):
        return nc.dram_tensor(name, list(shape), dt, kind="ExternalInput").ap()

    def dscr(name, shape, dt):
        kind = "ExternalOutput" if name in dbg else "Internal"
        return nc.dram_tensor(name, list(shape), dt, kind=kind).ap()

    I = {}
    I['rows'] = din('rows', [R, D])
    I['cT'] = din('cT', [128, 8, 2])
    for l in (0, 1):
        I[f'ada_w{l}'] = din(f'ada_w{l}', [D, 6 * D])
        I[f'ada_b{l}'] = din(f'ada_b{l}', [6 * D])
        I[f'norm1_{l}'] = din(f'norm1_{l}', [D])
        I[f'norm2_{l}'] = din(f'norm2_{l}', [D])
        I[f'w_out{l}'] = din(f'w_out{l}', [D, D])
        I[f'w1_{l}'] = din(f'w1_{l}', [D, 4 * D])
        I[f'w2_{l}'] = din(f'w2_{l}', [4 * D, D])
    I['w_in0'] = din('w_in0', [D, 2480])
    I['w_in1'] = din('w_in1', [D, 1536])
    I['q_norm0'] = din('q_norm0', [256])
    I['kv_norm0'] = din('kv_norm0', [128])
    I['w_uq'] = din('w_uq', [256, 768])
    I['w_ukv'] = din('w_ukv', [128, 1024])
    I['convT'] = din('convT', [128, 8, 3])
    I['gate_bT'] = din('gate_bT', [4, 4])
    I['head_norm'] = din('head_norm', [512])
    I['q_norm1'] = din('q_norm1', [128])
    I['k_norm1'] = din('k_norm1', [128])
    I['final_norm'] = din('final_norm', [D])
    I['ropeM'] = din('ropeM', [R, 2, 32])
    I['ropeG'] = din('ropeG', [R, 2, 128])
    I['sel'] = din('sel', [128, 2])
    y = nc.dram_tensor("y", [TH, D], F32, kind="ExternalOutput").ap()

    S = {}
    S['MODS'] = dscr('MODS', [2, 2, 6 * D], F32)

    with ExitStack() as st:
        P = Prog(nc, st)
        idb = P.sb(st, 'idb', [128, 128], BF16)
        idf = P.sb(st, 'idf', [128, 128], F32)
        P.pool(lambda e: e.memset(idb[:], 1.0), w=[idb])
        P.pool(lambda e: e.affine_select(out=idb[:], in_=idb[:], pattern=[[-1, 128]], base=0, channel_multiplier=1,
                                         compare_op=ALU.is_equal, fill=0.0), r=[idb], w=[idb])
        P.pool(lambda e: e.memset(idf[:], 1.0), w=[idf])
        P.pool(lambda e: e.affine_select(out=idf[:], in_=idf[:], pattern=[[-1, 128]], base=0, channel_multiplier=1,
                                         compare_op=ALU.is_equal, fill=0.0), r=[idf], w=[idf])
        PS = [P.ps(st, f'ps{i}', [128, 512], F32) for i in range(8)]
        C = K()
        C.nc, C.P, C.I, C.S, C.PS, C.idb, C.idf, C.y, C.dscr = nc, P, I, S, PS, idb, idf, y, dscr
        selt = P.sb(st, 'selt', [128, 2], F32)
        P.dma('sp', selt[:], I['sel'][:, :], selt, True)
        C.sel = selt

        C.limit_mt = limit_mt; C.limit_heads = limit_heads; C.limit_qb = limit_qb
        phase_mods(C)
        P.barrier(); P.flush()
        if stop_after == 'mods':
            return nc
        phase_a0(C)
        P.barrier(); P.flush()
        if stop_after == 'a0':
            return nc
        S['MIXT0'] = dscr('MIXT0', [1024, R], BF16)
        S['SN0'] = dscr('SN0', [R, D], F32)
        S['UT'] = dscr('UT', [4096, R], BF16)
        S['S1'] = dscr('S1', [R, D], F32)
        S['MIXT1'] = dscr('MIXT1', [1024, TH], BF16)
        S['SN1'] = dscr('SN1', [TH, D], F32)
        steps = [
            ('attn0', lambda: phase_attn(C, 'x_', S['QT0'], S['KT0'], S['V0'], 8, 8, 96, 64, float(96 ** -0.5), S['MIXT0'], True, merged=True)),
            ('mlprep', lambda: phase_ml_prep(C)),
            ('mlscan', lambda: phase_ml_scan(C)),
            ('mlout', lambda: phase_ml_out(C, S['MIXT0'])),
            ('f1_0', lambda: phase_f1(C, 'f_', 0, S['MIXT0'], I['rows'], S['SN0'], S['UT'], True)),
            ('f2_0', lambda: phase_f2(C, 'g_', 0, S['SN0'], S['UT'], S['S1'], True, False)),
            ('a1', lambda: phase_a1(C)),
            ('attn1', lambda: phase_attn(C, 'y_', S['QT1'], S['KT1'], S['V1'], 8, 2, 128, 128, float(128 ** -0.5), S['MIXT1'], False, split=True)),
            ('f1_1', lambda: phase_f1(C, 'h_', 1, S['MIXT1'], S['S1'], S['SN1'], S['UT'], False, split=True)),
            ('f2_1', lambda: phase_f2(C, 'i_', 1, S['SN1'], S['UT'], None, False, True, split=True)),
        ]
        for name, fn in steps:
            if skip and name in skip:
                continue
            fn()
            P.barrier(); P.flush()
            if stop_after == name:
                return nc
        P.barrier(); P.flush()
    return nc


def phase_mods(C):
    P, I, S, PS = C.P, C.I, C.S, C.PS
    with ExitStack() as ph:
        cT = P.sb(ph, 'm_cT', [128, 8, 2], F32)
        sc = P.sb(ph, 'm_sc', [128, 8, 2], BF16)
        wst = [P.sb(ph, f'm_wst{i}', [128, 3072], F32) for i in range(4)]
        wbf = [P.sb(ph, f'm_wbf{i}', [128, 3072], BF16) for i in range(4)]
        ab = P.sb(ph, 'm_ab', [2, 6 * D], F32)
        mods = P.sb(ph, 'm_mods', [2, 6 * D], F32)
        nrm = P.sb(ph, 'm_nrm', [2, 2, D], F32)
        P.dma('sp', cT[:], I['cT'][:, :, :], cT, True)
        P.act(lambda e: e.activation(out=sc[:], in_=cT[:], func=AF.Silu), r=[cT], w=[sc])
        it = 0
        for l in (0, 1):
            P.dma('sp', ab[:], I[f'ada_b{l}'].partition_broadcast(2), ab, True)
            P.dma('sp', nrm[:, 0, :], I[f'norm1_{l}'].partition_broadcast(2), nrm, True)
            P.dma('sp', nrm[:, 1, :], I[f'norm2_{l}'].partition_broadcast(2), nrm, True)
            for half in range(2):
                for k in range(8):
                    b = it % 4; it += 1
                    P.dma('sp', wst[b][:], I[f'ada_w{l}'][k * 128:(k + 1) * 128, half * 3072:(half + 1) * 3072], wst[b], True)
                    if it % 2 == 0:
                        P.act(lambda e, b=b: e.activation(out=wbf[b][:], in_=wst[b][:], func=AF.Copy), r=[wst[b]], w=[wbf[b]])
                    else:
                        P.dve(lambda e, b=b: e.tensor_copy(out=wbf[b][:], in_=wst[b][:]), r=[wst[b]], w=[wbf[b]])
                    for j in range(6):
                        P.pe(lambda e, b=b, j=j, k=k: e.matmul(PS[j][0:2, :], lhsT=sc[:, k, :], rhs=wbf[b][:, j * 512:(j + 1) * 512],
                                                               start=(k == 0), stop=(k == 7)), r=[sc, wbf[b]], w=[PS[j]])
                for j in range(6):
                    c0 = half * 3072 + j * 512
                    P.dve(lambda e, j=j, c0=c0: e.tensor_tensor(out=mods[:, c0:c0 + 512], in0=PS[j][0:2, :], in1=ab[:, c0:c0 + 512], op=ALU.add),
                          r=[PS[j], ab], w=[mods])
            for (cs, ni) in ((1, 0), (4, 1)):
                P.dve(lambda e, cs=cs, ni=ni: e.scalar_tensor_tensor(out=mods[:, cs * D:(cs + 1) * D], in0=mods[:, cs * D:(cs + 1) * D], scalar=1.0,
                                                                     in1=nrm[:, ni, :], op0=ALU.add, op1=ALU.mult), r=[mods, nrm], w=[mods])
            P.dma('sp', S['MODS'][l, :, :], mods[:], mods, False)


def load_w(C, ph, name, src, kc, ncol, stg, cast_eng='pool'):
    P = C.P
    w = P.sb(ph, name, [128, kc, ncol], BF16)
    sw = stg[0].shape[1]
    for k in range(kc):
        for c0 in range(0, ncol, sw):
            cw = min(sw, ncol - c0)
            s_ = stg[C.stg_i % len(stg)]; C.stg_i += 1
            P.dma('sp', s_[:, 0:cw], src[k * 128:(k + 1) * 128, c0:c0 + cw], s_, True)
            if C.stg_i % 2 == 0:
                P.act(lambda e, k=k, s_=s_, c0=c0, cw=cw: e.activation(out=w[:, k, c0:c0 + cw], in_=s_[:, 0:cw], func=AF.Copy), r=[s_], w=[w])
            else:
                P.dve(lambda e, k=k, s_=s_, c0=c0, cw=cw: e.tensor_copy(out=w[:, k, c0:c0 + cw], in_=s_[:, 0:cw]), r=[s_], w=[w])
    return w


def load_bcast(C, ph, name, src_ap, n, parts=128):
    t = C.P.sb(ph, name, [parts, n], F32)
    C.P.dma('sp', t[:], src_ap.partition_broadcast(parts), t, True)
    return t


def rstd_of(C, src_t, src_ap, n, junk, ss, out_rs):
    P = C.P
    P.act(lambda e: e.activation(out=junk[:, 0:n], in_=src_ap, func=AF.Square, accum_out=ss[:]), r=[src_t], w=[junk, ss])
    P.act(lambda e: e.activation(out=ss[:], in_=ss[:], func=AF.Sqrt, scale=1.0 / n, bias=EPS), r=[ss], w=[ss])
    P.dve(lambda e: e.reciprocal(out=out_rs[:], in_=ss[:]), r=[ss], w=[out_rs])


def rope_tm(C, src_t, src_ap, tab, nh, dr, dst_t, dst_ap, t1, t2):
    P = C.P
    nf = dr // 4
    cosb = tab[:, 0, :].unsqueeze(1).to_broadcast([128, nh, dr])
    v5 = lambda ap: ap.rearrange("p h (a b f) -> p h a b f", a=2, b=2)
    sin5 = tab[:, 1, :].rearrange("p (a b f) -> p a b f", a=2, b=2)
    a1 = t1[:, 0:nh * dr].rearrange("p (h d) -> p h d", h=nh)
    a2 = t2[:, 0:nh * dr].rearrange("p (h d) -> p h d", h=nh)
    P.dve(lambda e: e.tensor_tensor(out=a1, in0=src_ap, in1=cosb, op=ALU.mult), r=[src_t, tab], w=[t1])
    for b in range(2):
        sb_ = sin5[:, :, b, :].unsqueeze(1).to_broadcast([128, nh, 2, nf])
        P.dve(lambda e, b=b, sb_=sb_: e.tensor_tensor(out=v5(a2)[:, :, :, b, :], in0=v5(src_ap)[:, :, :, 1 - b, :], in1=sb_, op=ALU.mult),
              r=[src_t, tab], w=[t2])
    P.dve(lambda e: e.tensor_tensor(out=dst_ap, in0=a1, in1=a2, op=ALU.add), r=[t1, t2], w=[dst_t])


def norm_mod_rows(C, B, i, src_rows_ap, gm, sh, xmT, col0, track=None):
    P, PS = C.P, C.PS
    xin = B['xin'][i % 2]; xm = B['xm'][i % 2]
    P.dma('sp', xin[:], src_rows_ap, xin, True)
    rstd_of(C, xin, xin[:], D, B['junk'], B['ss'], B['rs'])
    P.dve(lambda e: e.scalar_tensor_tensor(out=B['tmp'][:], in0=xin[:], scalar=B['rs'][:, 0:1], in1=gm[:], op0=ALU.mult, op1=ALU.mult),
          r=[xin, B['rs'], gm], w=[B['tmp']])
    P.dve(lambda e: e.tensor_tensor(out=xm[:], in0=B['tmp'][:], in1=sh[:], op=ALU.add), r=[B['tmp'], sh], w=[xm])
    pb = PS[0][:].bitcast(BF16)
    for k in range(8):
        P.pe(lambda e, k=k: e.transpose(out=pb[:, k * 128:(k + 1) * 128], in_=xm[:, k * 128:(k + 1) * 128], identity=C.idb[:]),
             r=[xm, C.idb], w=[PS[0]])
    P.act(lambda e: e.activation(out=xmT[:, :, col0:col0 + 128], in_=pb.rearrange("p (k t) -> p k t", k=8), func=AF.Copy),
          r=[PS[0]], w=[track if track is not None else xmT])


def norm_bufs(C, ph, pfx):
    P = C.P
    return {'xin': [P.sb(ph, f'{pfx}xin{i}', [128, D], F32) for i in range(2)],
            'xm': [P.sb(ph, f'{pfx}xm{i}', [128, D], BF16) for i in range(2)],
            'tmp': P.sb(ph, f'{pfx}tmp', [128, D], F32), 'junk': P.sb(ph, f'{pfx}junk', [128, D], BF16),
            'ss': P.sb(ph, f'{pfx}ss', [128, 1], F32), 'rs': P.sb(ph, f'{pfx}rs', [128, 1], F32)}


def mtiles():
    out = [(0, CT, True)]
    for m in range(T // 512):
        out.append((CT + m * 512, 512, False))
    return out


def phase_a0(C):
    P, I, S, PS = C.P, C.I, C.S, C.PS
    S['QKPRE'] = C.dscr('QKPRE', [1024, R], F32)
    S['GATES'] = C.dscr('GATES', [16, R], F32)
    S['VML'] = C.dscr('VML', [R, 512], BF16)
    S['OSIG'] = C.dscr('OSIG', [R, 512], F32)
    S['QT0'] = C.dscr('QT0', [8, 96, R], BF16)
    S['KT0'] = C.dscr('KT0', [8, 96, R], BF16)
    S['V0'] = C.dscr('V0', [R, 8, 128], BF16)
    with ExitStack() as ph:
        stg = [P.sb(ph, f'a_stg{i}', [128, 2480], F32) for i in range(2)]
        C.stg_i = 0
        w_in = load_w(C, ph, 'a_win', I['w_in0'], 8, 2480, stg)
        w_uq = load_w(C, ph, 'a_wuq', I['w_uq'], 2, 768, stg)
        w_ukv = load_w(C, ph, 'a_wukv', I['w_ukv'], 1, 1024, stg)
        gm = [load_bcast(C, ph, f'a_gm{j}', S['MODS'][0, j, 1 * D:2 * D], D) for j in range(2)]
        sh = [load_bcast(C, ph, f'a_sh{j}', S['MODS'][0, j, 0 * D:1 * D], D) for j in range(2)]
        qn = load_bcast(C, ph, 'a_qn', I['q_norm0'], 256)
        kvn = load_bcast(C, ph, 'a_kvn', I['kv_norm0'], 128)
        B = norm_bufs(C, ph, 'a_')
        xmT = [P.sb(ph, f'a_xmT{i}', [128, 8, 512], BF16) for i in range(2)]
        lat = P.sb(ph, 'a_lat', [128, 416], F32)
        latn = P.sb(ph, 'a_latn', [128, 384], BF16)
        latT = P.sb(ph, 'a_latT', [128, 3, 128], BF16)
        rq = P.sb(ph, 'a_rq', [128, 1], F32); rkv = P.sb(ph, 'a_rkv', [128, 1], F32)
        tab = [P.sb(ph, f'a_tab{i}', [128, 2, 32], F32) for i in range(2)]
        t1 = P.sb(ph, 'a_t1', [128, 256], F32); t2 = P.sb(ph, 'a_t2', [128, 256], F32)
        kper = P.sb(ph, 'a_kper', [128, 32], BF16)
        q_tm = P.sb(ph, 'a_qtm', [128, 8, 96], BF16); k_tm = P.sb(ph, 'a_ktm', [128, 8, 96], BF16)
        v0 = [P.sb(ph, f'a_v0{i}', [128, 8, 128], BF16) for i in range(2)]
        vml = [P.sb(ph, f'a_vml{i}', [128, 512], BF16) for i in range(2)]
        osg = [P.sb(ph, f'a_osg{i}', [128, 512], F32) for i in range(2)]
        qT = [P.sb(ph, f'a_qT{i}', [96, 8, 128], BF16) for i in range(2)]
        kT = [P.sb(ph, f'a_kT{i}', [96, 8, 128], BF16) for i in range(2)]
        fm = [P.sb(ph, f'a_fm{i}', [128, 512], F32) for i in range(2)]
        gsb = [P.sb(ph, f'a_g{i}', [16, 512], F32) for i in range(2)]
        for i in range(2):
            P.pool(lambda e, i=i: e.memset(v0[i][:], 1.0), w=[v0[i]])
        mts = mtiles()
        if C.limit_mt is not None:
            mts = mts[:C.limit_mt]
        tiles = [(mi, r) for mi, (row0, W, isctx) in enumerate(mts) for r in range(W // 128)]
        xv = [[Tl(xmT[i].t, f'a_xv{i}{r}') for r in range(4)] for i in range(2)]

        def stageA(ti):
            mi, r = tiles[ti]; row0, W, isctx = mts[mi]; j = 1 if isctx else 0
            rows = slice(row0 + r * 128, row0 + (r + 1) * 128)
            norm_mod_rows(C, B, ti, I['rows'][rows, :], gm[j], sh[j], xmT[mi % 2], r * 128, track=xv[mi % 2][r])
            tb = tab[ti % 2]
            P.dma('sp', tb[:], I['ropeM'][rows, :, :], tb, True)

        def stageB(ti):
            mi, r = tiles[ti]; row0, W, isctx = mts[mi]; j = 1 if isctx else 0
            rows = slice(row0 + r * 128, row0 + (r + 1) * 128)
            X = xmT[mi % 2]; XV = xv[mi % 2][r]; tb = tab[ti % 2]
            lhs = lambda k: X[:, k, r * 128:(r + 1) * 128]
            for k in range(8):
                P.pe(lambda e, k=k, l_=lhs(k): e.matmul(PS[1][:, 0:416], lhsT=l_, rhs=w_in[:, k, 0:416], start=(k == 0), stop=(k == 7)),
                     r=[XV, w_in], w=[PS[1]])
            P.dve(lambda e: e.tensor_copy(out=lat[:], in_=PS[1][:, 0:416]), r=[PS[1]], w=[lat])
            for k in range(8):
                P.pe(lambda e, k=k, l_=lhs(k): e.matmul(PS[2][:], lhsT=l_, rhs=w_in[:, k, 1440:1952], start=(k == 0), stop=(k == 7)),
                     r=[XV, w_in], w=[PS[2]])
            vm = vml[ti % 2]
            P.act(lambda e, vm=vm: e.activation(out=vm[:], in_=PS[2][:], func=AF.Copy), r=[PS[2]], w=[vm])
            P.dma('sp', S['VML'][rows, :], vm[:], vm, False)
            for k in range(8):
                P.pe(lambda e, k=k, l_=lhs(k): e.matmul(PS[3][:], lhsT=l_, rhs=w_in[:, k, 1952:2464], start=(k == 0), stop=(k == 7)),
                     r=[XV, w_in], w=[PS[3]])
            og = osg[ti % 2]
            P.act(lambda e, og=og: e.activation(out=og[:], in_=PS[3][:], func=AF.Sigmoid), r=[PS[3]], w=[og])
            P.dma('sp', S['OSIG'][rows, :], og[:], og, False)
            rstd_of(C, lat, lat[:, 0:256], 256, B['junk'], B['ss'], rq)
            rstd_of(C, lat, lat[:, 256:384], 128, B['junk'], B['ss'], rkv)
            P.dve(lambda e: e.scalar_tensor_tensor(out=latn[:, 0:256], in0=lat[:, 0:256], scalar=rq[:, 0:1], in1=qn[:], op0=ALU.mult, op1=ALU.mult),
                  r=[lat, rq, qn], w=[latn])
            P.dve(lambda e: e.scalar_tensor_tensor(out=latn[:, 256:384], in0=lat[:, 256:384], scalar=rkv[:, 0:1], in1=kvn[:], op0=ALU.mult, op1=ALU.mult),
                  r=[lat, rkv, kvn], w=[latn])
            pb = PS[4][:].bitcast(BF16)
            for k in range(3):
                P.pe(lambda e, k=k: e.transpose(out=pb[:, k * 128:(k + 1) * 128], in_=latn[:, k * 128:(k + 1) * 128], identity=C.idb[:]),
                     r=[latn, C.idb], w=[PS[4]])
            P.act(lambda e: e.activation(out=latT[:], in_=pb[:, 0:384].rearrange("p (k t) -> p k t", k=3), func=AF.Copy), r=[PS[4]], w=[latT])
            for kc in range(2):
                P.pe(lambda e, kc=kc: e.matmul(PS[5][:], lhsT=latT[:, kc, :], rhs=w_uq[:, kc, 0:512], start=(kc == 0), stop=(kc == 1)),
                     r=[latT, w_uq], w=[PS[5]])
            for kc in range(2):
                P.pe(lambda e, kc=kc: e.matmul(PS[6][:, 0:256], lhsT=latT[:, kc, :], rhs=w_uq[:, kc, 512:768], start=(kc == 0), stop=(kc == 1)),
                     r=[latT, w_uq], w=[PS[6]])
            P.pe(lambda e: e.matmul(PS[7][:], lhsT=latT[:, 2, :], rhs=w_ukv[:, 0, 0:512], start=True, stop=True), r=[latT, w_ukv], w=[PS[7]])
            P.pe(lambda e: e.matmul(PS[2][:], lhsT=latT[:, 2, :], rhs=w_ukv[:, 0, 512:1024], start=True, stop=True), r=[latT, w_ukv], w=[PS[2]])
            P.act(lambda e: e.activation(out=q_tm[:, :, 0:64], in_=PS[5][:].rearrange("p (h d) -> p h d", h=8), func=AF.Copy), r=[PS[5]], w=[q_tm])
            rope_tm(C, PS[6], PS[6][:, 0:256].rearrange("p (h d) -> p h d", h=8), tb, 8, 32, q_tm, q_tm[:, :, 64:96], t1, t2)
            P.act(lambda e: e.activation(out=k_tm[:, :, 0:64], in_=PS[7][:].rearrange("p (h d) -> p h d", h=8), func=AF.Copy), r=[PS[7]], w=[k_tm])
            rope_tm(C, lat, lat[:, 384:416].unsqueeze(1), tb, 1, 32, kper, kper[:].unsqueeze(1), t1, t2)
            P.dve(lambda e: e.tensor_copy(out=k_tm[:, :, 64:96], in_=kper[:].unsqueeze(1).to_broadcast([128, 8, 32])), r=[kper], w=[k_tm])
            vv = v0[ti % 2]
            P.act(lambda e, vv=vv: e.activation(out=vv[:, :, 0:64], in_=PS[2][:].rearrange("p (h d) -> p h d", h=8), func=AF.Copy), r=[PS[2]], w=[vv])
            P.dma('sp', S['V0'][rows, :, :], vv[:], vv, False)
            for (src, dstl, bank, dname) in ((q_tm, qT, 3, 'QT0'), (k_tm, kT, 1, 'KT0')):
                pbh = PS[bank][:].bitcast(BF16)
                for h in range(8):
                    P.pe(lambda e, h=h, src=src, pbh=pbh: e.transpose(out=pbh[0:96, h * 128:(h + 1) * 128], in_=src[:, h, :], identity=C.idb[:]),
                         r=[src, C.idb], w=[PS[bank]])
                dt_ = dstl[ti % 2]
                P.dve(lambda e, dt_=dt_, pbh=pbh: e.tensor_copy(out=dt_[:], in_=pbh[0:96, :].rearrange("p (h t) -> p h t", h=8)), r=[PS[bank]], w=[dt_])
                P.dma('sp', S[dname][:, :, rows].rearrange("h d r -> d h r"), dt_[:], dt_, False)

        def stageC(mi):
            row0, W, isctx = mts[mi]
            X = xmT[mi % 2]; XVs = xv[mi % 2][:W // 128]
            cols = slice(row0, row0 + W)
            for ch in range(8):
                bank = 5 + (ch % 2)
                for k in range(8):
                    P.pe(lambda e, k=k, ch=ch, bank=bank, X=X, W=W: e.matmul(PS[bank][:, 0:W], lhsT=w_in[:, k, 416 + ch * 128:416 + (ch + 1) * 128], rhs=X[:, k, 0:W],
                                                                     start=(k == 0), stop=(k == 7)), r=XVs + [w_in], w=[PS[bank]])
                f_ = fm[ch % 2]
                P.dve(lambda e, f_=f_, bank=bank, W=W: e.tensor_copy(out=f_[:, 0:W], in_=PS[bank][:, 0:W]), r=[PS[bank]], w=[f_])
                P.dma('sp', S['QKPRE'][ch * 128:(ch + 1) * 128, cols], f_[:, 0:W], f_, False)
            for k in range(8):
                P.pe(lambda e, k=k, X=X, W=W: e.matmul(PS[7][0:16, 0:W], lhsT=w_in[:, k, 2464:2480], rhs=X[:, k, 0:W], start=(k == 0), stop=(k == 7)),
                     r=XVs + [w_in], w=[PS[7]])
            g_ = gsb[mi % 2]
            P.dve(lambda e, g_=g_, W=W: e.tensor_copy(out=g_[:, 0:W], in_=PS[7][0:16, 0:W]), r=[PS[7]], w=[g_])
            P.dma('sp', S['GATES'][:, cols], g_[:, 0:W], g_, False)

        stageA(0)
        for ti in range(len(tiles)):
            if ti + 1 < len(tiles):
                stageA(ti + 1)
            stageB(ti)
            mi, r = tiles[ti]
            if r == mts[mi][1] // 128 - 1:
                stageC(mi)


def phase_attn(C, pfx, QT, KT, V, nh, nkv, d, dv, scale, MIXT, with_ctx_q, merged=False, split=False):
    P, PS = C.P, C.PS
    grp = nh // nkv
    with ExitStack() as ph:
        kt = [P.sb(ph, f'{pfx}kt{i}', [128, R], BF16) for i in range(2)]
        dvl = 128 if merged else dv
        vt = [P.sb(ph, f'{pfx}vt{i}', [128, R // 128, dvl], BF16) for i in range(2)]
        sel = P.sb(ph, f'{pfx}sel', [128, 128], F32)
        rr = P.sb(ph, f'{pfx}rr', [128, 512], F32)
        if merged:
            P.pool(lambda e: e.memset(sel[:], 1.0), w=[sel])
            P.pool(lambda e: e.affine_select(out=sel[:], in_=sel[:], pattern=[[-1, 128]], base=-dv, channel_multiplier=1, compare_op=ALU.is_equal, fill=0.0),
                   r=[sel], w=[sel])
        qt = [P.sb(ph, f'{pfx}qt{i}', [128, 512], BF16) for i in range(2)]
        pt = [P.sb(ph, f'{pfx}pt{i}', [128, 512], BF16) for i in range(6)]
        ones = P.sb(ph, f'{pfx}ones', [128, 128], BF16)
        rl = P.sb(ph, f'{pfx}rl', [128, 512], F32)
        ot = [P.sb(ph, f'{pfx}ot{i}', [128, 512], BF16) for i in range(2)]
        P.pool(lambda e: e.memset(ones[:], 1.0), w=[ones])
        dve_l = (not merged) and DVE_L
        NSB = 4 if not dve_l else 3
        if dve_l:
            ones_f = P.sb(ph, f'{pfx}onesf', [128, 128], F32)
            accs = P.sb(ph, f'{pfx}accs', [128, 512], F32)
            P.pool(lambda e: e.memset(ones_f[:], 1.0), w=[ones_f])
            ab = PS[3]
        qblocks = []
        if with_ctx_q:
            qblocks.append((0, CT, CT // 128))
        if split:
            qa_t = [P.sb(ph, f'{pfx}qa{i}', [128, 512], BF16) for i in range(2)]
            qb_t = [P.sb(ph, f'{pfx}qb{i}', [128, 512], BF16) for i in range(2)]
            qtmp = P.sb(ph, f'{pfx}qtmp', [128, 512], F32)
            for m in range(TH // 512):
                qblocks.append((m * 512, 512, R // 128))
        else:
            for m in range(T // 512):
                qblocks.append((CT + m * 512, 512, R // 128))
        if C.limit_qb is not None:
            qblocks = qblocks[:C.limit_qb]
        qi = 0; pi = 0; si = 0
        pending = []
        for g in range(nkv):
            if C.limit_heads is not None and g * grp >= C.limit_heads:
                break
            kb = kt[g % 2]; vb = vt[g % 2]
            P.dma('sp', kb[0:d, :], KT[g, :, :], kb, True)
            P.dma('sp', vb[:], V[:, g, 0:dvl].rearrange("(j p) c -> p j c", p=128), vb, True)
            for hh in range(grp):
                h = g * grp + hh
                if C.limit_heads is not None and h >= C.limit_heads:
                    break
                for (q0, W, nkt) in qblocks:
                    qb = qt[qi % 2]
                    if split:
                        qa_ = qa_t[qi % 2]; qb_ = qb_t[qi % 2]
                        P.dma('sp', qa_[0:d, 0:W], QT[h, :, CT + q0:CT + q0 + W], qa_, True)
                        P.dma('sp', qb_[0:d, 0:W], QT[h, :, CT + TH + q0:CT + TH + q0 + W], qb_, True)
                        P.dve(lambda e, qa_=qa_, W=W: e.tensor_scalar(out=qtmp[0:d, 0:W], in0=qa_[0:d, 0:W], scalar1=C.sel[0:d, 0:1], scalar2=None, op0=ALU.mult),
                              r=[qa_, C.sel], w=[qtmp])
                        P.dve(lambda e, qb_=qb_, qb=qb, W=W: e.scalar_tensor_tensor(out=qb[0:d, 0:W], in0=qb_[0:d, 0:W], scalar=C.sel[0:d, 1:2], in1=qtmp[0:d, 0:W], op0=ALU.mult, op1=ALU.add),
                              r=[qb_, C.sel, qtmp], w=[qb])
                    else:
                        P.dma('sp', qb[0:d, 0:W], QT[h, :, q0:q0 + W], qb, True)
                    ob = PS[4 + 2 * (qi % 2)]; lb = PS[5 + 2 * (qi % 2)]

                    def rec_s(j, qb=qb, W=W, kb=kb):
                        nonlocal si
                        bank = PS[si % NSB]; si += 1
                        P.pe(lambda e, bank=bank, j=j: e.matmul(bank[:, 0:W], lhsT=kb[0:d, j * 128:(j + 1) * 128], rhs=qb[0:d, 0:W], start=True, stop=True),
                             r=[kb, qb], w=[bank])
                        return bank

                    def rec_pv(j, bank, W=W, vb=vb, ob=ob, lb=lb, nkt=nkt):
                        nonlocal pi
                        pb = pt[pi % 6]; pi += 1
                        P.act(lambda e, pb=pb, bank=bank: e.activation(out=pb[:, 0:W], in_=bank[:, 0:W], func=AF.Exp, scale=scale), r=[bank], w=[pb])
                        P.pe(lambda e, pb=pb, j=j: e.matmul(ob[0:dvl, 0:W], lhsT=vb[:, j, :], rhs=pb[:, 0:W], start=(j == 0), stop=(j == nkt - 1)),
                             r=[vb, pb], w=[ob])
                        if dve_l:
                            if j % 3 == 0:
                                P.pe(lambda e, pb=pb, j=j: e.matmul(lb[0:dv, 0:W], lhsT=ones[:, 0:dv], rhs=pb[:, 0:W], start=(j == 0), stop=False),
                                     r=[ones, pb], w=[lb])
                            elif j == 1:
                                P.dve(lambda e, pb=pb: e.tensor_copy(out=ab[:, 0:W], in_=pb[:, 0:W]), r=[pb], w=[ab])
                            else:
                                P.dve(lambda e, pb=pb: e.tensor_tensor(out=ab[:, 0:W], in0=ab[:, 0:W], in1=pb[:, 0:W], op=ALU.add), r=[ab, pb], w=[ab])
                        elif not merged:
                            P.pe(lambda e, pb=pb, j=j: e.matmul(lb[0:dv, 0:W], lhsT=ones[:, 0:dv], rhs=pb[:, 0:W], start=(j == 0), stop=(j == nkt - 1)),
                                 r=[ones, pb], w=[lb])

                    banks = {}
                    LA = NSB - 1
                    for j in range(min(LA, nkt)):
                        banks[j] = rec_s(j)
                    for j in range(nkt):
                        if j + LA < nkt:
                            banks[j + LA] = rec_s(j + LA)
                        rec_pv(j, banks.pop(j))
                        if j == min(EPI_DEFER, nkt - 1) and pending:
                            pending.pop(0)()
                    o_ = ot[qi % 2]
                    if dve_l:
                        P.act(lambda e, W=W: e.activation(out=accs[:, 0:W], in_=ab[:, 0:W], func=AF.Copy), r=[ab], w=[accs])

                    def epi(ob=ob, lb=lb, o_=o_, W=W, h=h, q0=q0):
                        if merged:
                            P.act(lambda e: e.activation(out=rr[0:dv, 0:W], in_=ob[0:dv, 0:W], func=AF.Copy), r=[ob], w=[rr])
                            P.dve(lambda e: e.reciprocal(out=rr[dv:128, 0:W], in_=ob[dv:128, 0:W]), r=[ob], w=[rr])
                            P.pe(lambda e: e.matmul(lb[:, 0:W], lhsT=sel[:], rhs=rr[:, 0:W], start=True, stop=True), r=[sel, rr], w=[lb])
                            P.dve(lambda e: e.tensor_tensor(out=o_[0:dv, 0:W], in0=rr[0:dv, 0:W], in1=lb[0:dv, 0:W], op=ALU.mult), r=[rr, lb], w=[o_])
                        else:
                            if dve_l:
                                P.pe(lambda e: e.matmul(lb[0:dv, 0:W], lhsT=ones_f[:, 0:dv], rhs=accs[:, 0:W], start=False, stop=True), r=[ones_f, accs], w=[lb])
                            P.dve(lambda e: e.reciprocal(out=rl[0:dv, 0:W], in_=lb[0:dv, 0:W]), r=[lb], w=[rl])
                            P.dve(lambda e: e.tensor_tensor(out=o_[0:dv, 0:W], in0=ob[0:dv, 0:W], in1=rl[0:dv, 0:W], op=ALU.mult), r=[ob, rl], w=[o_])
                        P.dma('sp', MIXT[h * dv:(h + 1) * dv, q0:q0 + W], o_[0:dv, 0:W], o_, False)
                    pending.append(epi)
                    qi += 1

        while pending:
            pending.pop(0)()

def phase_f1(C, pfx, l, MIXT, SRC, SN, UT, with_ctx, split=False):
    P, I, S, PS = C.P, C.I, C.S, C.PS
    with ExitStack() as ph:
        stg = [P.sb(ph, f'{pfx}stg{i}', [128, 1024], F32) for i in range(2)]
        C.stg_i = 0
        w_out = load_w(C, ph, f'{pfx}wo', I[f'w_out{l}'], 8, 1024, stg)
        w1 = load_w(C, ph, f'{pfx}w1', I[f'w1_{l}'], 8, 4096, stg)
        nj = 2 if with_ctx else 1
        g1 = [load_bcast(C, ph, f'{pfx}g1{j}', S['MODS'][l, j, 2 * D:3 * D], D) for j in range(nj)]
        sh = [load_bcast(C, ph, f'{pfx}sh{j}', S['MODS'][l, j, 3 * D:4 * D], D) for j in range(nj)]
        gm = [load_bcast(C, ph, f'{pfx}gm{j}', S['MODS'][l, j, 4 * D:5 * D], D) for j in range(nj)]
        mx = [P.sb(ph, f'{pfx}mx{i}', [128, 8, 512], BF16) for i in range(2)]
        sin = [P.sb(ph, f'{pfx}sin{i}', [128, D], F32) for i in range(2)]
        sn = [P.sb(ph, f'{pfx}sn{i}', [128, D], F32) for i in range(2)]
        hm = [P.sb(ph, f'{pfx}hm{i}', [128, D], BF16) for i in range(2)]
        tmp = P.sb(ph, f'{pfx}tmp', [128, D], F32)
        junk = P.sb(ph, f'{pfx}junk', [128, D], BF16)
        ss = P.sb(ph, f'{pfx}ss', [128, 1], F32); rs = P.sb(ph, f'{pfx}rs', [128, 1], F32)
        hT = [P.sb(ph, f'{pfx}hT{i}', [128, 8, 512], BF16) for i in range(2)]
        rl_ = [P.sb(ph, f'{pfx}rl{i}', [128, 512], F32) for i in range(2)]
        ut = [P.sb(ph, f'{pfx}ut{i}', [128, 512], BF16) for i in range(3)]
        ti = 0; ui = 0
        mts = mtiles() if with_ctx else mtiles()[1:]
        if split:
            mts = [(m * 512, 512, False) for m in range(TH // 512)]
            sin2 = [P.sb(ph, f'{pfx}sinb{i}', [128, D], F32) for i in range(2)]
        if C.limit_mt is not None:
            mts = mts[:C.limit_mt]
        state = {'ti': 0, 'ui': 0}

        def rowtile(mi, r):
            row0, W, isctx = mts[mi]
            j = 1 if isctx else 0
            M_ = mx[mi % 2]; H = hT[mi % 2]
            ti = state['ti']; state['ti'] += 1
            if r == 0:
                P.dma('sp', M_[:, :, 0:W], MIXT[:, row0:row0 + W].rearrange("(k p) r -> p k r", p=128), M_, True)
            rows = slice(row0 + r * 128, row0 + (r + 1) * 128)
            si_ = sin[ti % 2]; sn_ = sn[ti % 2]; hm_ = hm[ti % 2]
            if split:
                sb2 = sin2[ti % 2]
                ra = slice(CT + rows.start, CT + rows.stop); rb = slice(CT + TH + rows.start, CT + TH + rows.stop)
                P.dma('sp', si_[:], SRC[ra, :], si_, True)
                P.dma('sp', sb2[:], SRC[rb, :], sb2, True)
                P.dve(lambda e: e.tensor_scalar(out=si_[:], in0=si_[:], scalar1=C.sel[:, 0:1], scalar2=None, op0=ALU.mult), r=[si_, C.sel], w=[si_])
                P.dve(lambda e: e.scalar_tensor_tensor(out=si_[:], in0=sb2[:], scalar=C.sel[:, 1:2], in1=si_[:], op0=ALU.mult, op1=ALU.add),
                      r=[sb2, C.sel, si_], w=[si_])
            else:
                P.dma('sp', si_[:], SRC[rows, :], si_, True)
            for half in range(2):
                bank = PS[1 + half]
                for k in range(8):
                    P.pe(lambda e, k=k, half=half, bank=bank: e.matmul(bank[:], lhsT=M_[:, k, r * 128:(r + 1) * 128], rhs=w_out[:, k, half * 512:(half + 1) * 512],
                                                                     start=(k == 0), stop=(k == 7)), r=[M_, w_out], w=[bank])
                cs = slice(half * 512, (half + 1) * 512)
                P.dve(lambda e, bank=bank, cs=cs: e.tensor_tensor(out=tmp[:, cs], in0=bank[:], in1=g1[j][:, cs], op=ALU.mult), r=[bank, g1[j]], w=[tmp])
            P.dve(lambda e: e.tensor_tensor(out=sn_[:], in0=tmp[:], in1=si_[:], op=ALU.add), r=[tmp, si_], w=[sn_])
            P.dma('sp', SN[rows, :], sn_[:], sn_, False)
            P.act(lambda e: e.activation(out=junk[:], in_=sn_[:], func=AF.Square, accum_out=ss[:]), r=[sn_], w=[junk, ss])
            P.act(lambda e: e.activation(out=ss[:], in_=ss[:], func=AF.Sqrt, scale=1.0 / D, bias=EPS), r=[ss], w=[ss])
            P.dve(lambda e: e.reciprocal(out=rs[:], in_=ss[:]), r=[ss], w=[rs])
            P.dve(lambda e: e.scalar_tensor_tensor(out=tmp[:], in0=sn_[:], scalar=rs[:, 0:1], in1=gm[j][:], op0=ALU.mult, op1=ALU.mult),
                  r=[sn_, rs, gm[j]], w=[tmp])
            P.dve(lambda e: e.tensor_tensor(out=hm_[:], in0=tmp[:], in1=sh[j][:], op=ALU.add), r=[tmp, sh[j]], w=[hm_])
            pb = PS[0][:].bitcast(BF16)
            for k in range(8):
                P.pe(lambda e, k=k: e.transpose(out=pb[:, k * 128:(k + 1) * 128], in_=hm_[:, k * 128:(k + 1) * 128], identity=C.idb[:]),
                     r=[hm_, C.idb], w=[PS[0]])
            P.act(lambda e: e.activation(out=H[:, :, r * 128:(r + 1) * 128], in_=pb.rearrange("p (k t) -> p k t", k=8), func=AF.Copy),
                  r=[PS[0]], w=[H])

        def ffn_up(mi, f):
            row0, W, isctx = mts[mi]
            H = hT[mi % 2]
            bank = PS[3 + (f % 4)]
            for k in range(8):
                P.pe(lambda e, k=k: e.matmul(bank[:, 0:W], lhsT=w1[:, k, f * 128:(f + 1) * 128], rhs=H[:, k, 0:W], start=(k == 0), stop=(k == 7)),
                     r=[H, w1], w=[bank])
            ui = state['ui']; state['ui'] += 1
            r_ = rl_[f % 2]; u_ = ut[ui % 3]
            P.act(lambda e: e.activation(out=r_[:, 0:W], in_=bank[:, 0:W], func=AF.Relu), r=[bank], w=[r_])
            P.dve(lambda e: e.tensor_tensor(out=u_[:, 0:W], in0=r_[:, 0:W], in1=r_[:, 0:W], op=ALU.mult), r=[r_], w=[u_])
            P.dma('sp', UT[f * 128:(f + 1) * 128, row0:row0 + W], u_[:, 0:W], u_, False)

        for r in range(mts[0][1] // 128):
            rowtile(0, r)
        for mi in range(len(mts)):
            nxt = (mts[mi + 1][1] // 128) if mi + 1 < len(mts) else 0
            for q in range(4):
                for f in range(8 * q, 8 * q + 8):
                    ffn_up(mi, f)
                if q < nxt:
                    rowtile(mi + 1, q)


def phase_f2(C, pfx, l, SN, UT, DST, with_ctx, final, split=False):
    P, I, S, PS = C.P, C.I, C.S, C.PS
    with ExitStack() as ph:
        stg = [P.sb(ph, f'{pfx}stg{i}', [128, 1024], F32) for i in range(2)]
        C.stg_i = 0
        w2 = load_w(C, ph, f'{pfx}w2', I[f'w2_{l}'], 32, 1024, stg)
        nj = 2 if with_ctx else 1
        g2 = [load_bcast(C, ph, f'{pfx}g2{j}', S['MODS'][l, j, 5 * D:6 * D], D) for j in range(nj)]
        fn = load_bcast(C, ph, f'{pfx}fn', I['final_norm'], D) if final else None
        ub = [P.sb(ph, f'{pfx}ub{i}', [128, 32, 256], BF16) for i in range(2)]
        sn = [P.sb(ph, f'{pfx}sn{i}', [128, D], F32) for i in range(2)]
        so = [P.sb(ph, f'{pfx}so{i}', [128, D], F32) for i in range(2)]
        tmp = P.sb(ph, f'{pfx}tmp', [128, D], F32)
        junk = P.sb(ph, f'{pfx}junk', [128, D], BF16)
        ss = P.sb(ph, f'{pfx}ss', [128, 1], F32); rs = P.sb(ph, f'{pfx}rs', [128, 1], F32)
        ti = 0
        mts = [(r0, 256, r0 < CT) for r0 in range(0 if with_ctx else CT, R, 256)]
        yoff = -CT
        if split:
            mts = [(r0, 256, False) for r0 in range(0, TH, 256)]; yoff = 0
        if C.limit_mt is not None:
            mts = mts[:2 * C.limit_mt - (1 if with_ctx else 0)]
        for mi, (row0, W, isctx) in enumerate(mts):
            j = 1 if isctx else 0
            U = ub[mi % 2]
            for fq in range(4):
                P.dma('sp', U[:, fq * 8:(fq + 1) * 8, 0:W], UT[fq * 1024:(fq + 1) * 1024, row0:row0 + W].rearrange("(f p) r -> p f r", p=128), U, True)
            for r in range(W // 128):
                rows = slice(row0 + r * 128, row0 + (r + 1) * 128)
                sn_ = sn[ti % 2]; so_ = so[ti % 2]
                P.dma('sp', sn_[:], SN[rows, :], sn_, True)
                for half in range(2):
                    bank = PS[2 * (ti % 2) + half]
                    for f in range(32):
                        P.pe(lambda e, f=f, half=half, bank=bank, U=U, r=r: e.matmul(bank[:], lhsT=U[:, f, r * 128:(r + 1) * 128], rhs=w2[:, f, half * 512:(half + 1) * 512],
                                                                              start=(f == 0), stop=(f == 31)), r=[U, w2], w=[bank])
                    cs = slice(half * 512, (half + 1) * 512)
                    P.dve(lambda e, bank=bank, cs=cs, j=j: e.tensor_tensor(out=tmp[:, cs], in0=bank[:], in1=g2[j][:, cs], op=ALU.mult), r=[bank, g2[j]], w=[tmp])
                P.dve(lambda e, sn_=sn_, so_=so_: e.tensor_tensor(out=so_[:], in0=tmp[:], in1=sn_[:], op=ALU.add), r=[tmp, sn_], w=[so_])
                if final:
                    P.act(lambda e, so_=so_: e.activation(out=junk[:], in_=so_[:], func=AF.Square, accum_out=ss[:]), r=[so_], w=[junk, ss])
                    P.act(lambda e: e.activation(out=ss[:], in_=ss[:], func=AF.Sqrt, scale=1.0 / D, bias=EPS), r=[ss], w=[ss])
                    P.dve(lambda e: e.reciprocal(out=rs[:], in_=ss[:]), r=[ss], w=[rs])
                    P.dve(lambda e, so_=so_: e.scalar_tensor_tensor(out=so_[:], in0=so_[:], scalar=rs[:, 0:1], in1=fn[:], op0=ALU.mult, op1=ALU.mult),
                          r=[so_, rs, fn], w=[so_])
                    P.dma('sp', C.y[row0 + yoff + r * 128:row0 + yoff + (r + 1) * 128, :], so_[:], so_, False)
                else:
                    P.dma('sp', DST[rows, :], so_[:], so_, False)
                ti += 1


def phase_a1(C):
    P, I, S, PS = C.P, C.I, C.S, C.PS
    S['QT1'] = C.dscr('QT1', [8, 128, R], BF16)
    S['KT1'] = C.dscr('KT1', [2, 128, R], BF16)
    S['V1'] = C.dscr('V1', [R, 2, 128], BF16)
    with ExitStack() as ph:
        stg = [P.sb(ph, f'b_stg{i}', [128, 1536], F32) for i in range(2)]
        C.stg_i = 0
        w_in = load_w(C, ph, 'b_win', I['w_in1'], 8, 1536, stg)
        gm = [load_bcast(C, ph, f'b_gm{j}', S['MODS'][1, j, 1 * D:2 * D], D) for j in range(2)]
        sh = [load_bcast(C, ph, f'b_sh{j}', S['MODS'][1, j, 0 * D:1 * D], D) for j in range(2)]
        qn = load_bcast(C, ph, 'b_qn', I['q_norm1'], 128)
        kn = load_bcast(C, ph, 'b_kn', I['k_norm1'], 128)
        B = norm_bufs(C, ph, 'b_')
        Xs = [P.sb(ph, f'b_xmT{i}', [128, 8, 128], BF16) for i in range(2)]
        tab = [P.sb(ph, f'b_tab{i}', [128, 2, 128], F32) for i in range(3)]
        qfs = [P.sb(ph, f'b_qf{i}', [128, 10, 128], F32) for i in range(2)]
        sq = P.sb(ph, 'b_sq', [128, 10, 128], F32)
        ssq = P.sb(ph, 'b_ssq', [128, 10], F32); rr = P.sb(ph, 'b_rr', [128, 10], F32)
        t1 = P.sb(ph, 'b_t1', [128, 1280], F32); t2 = P.sb(ph, 'b_t2', [128, 1280], F32)
        qk_tm = P.sb(ph, 'b_qktm', [128, 10, 128], BF16)
        v1 = [P.sb(ph, f'b_v1{i}', [128, 256], BF16) for i in range(2)]
        qkT = [P.sb(ph, f'b_qkT{i}', [128, 10, 128], BF16) for i in range(2)]
        nrt = R // 128
        if C.limit_mt is not None:
            nrt = 2 + 4 * (C.limit_mt - 1)
        def stageA(ti):
            rows = slice(ti * 128, (ti + 1) * 128)
            j = 1 if ti < 2 else 0
            norm_mod_rows(C, B, ti, S['S1'][rows, :], gm[j], sh[j], Xs[ti % 2], 0)
            tb = tab[ti % 3]
            P.dma('sp', tb[:], I['ropeG'][rows, :, :], tb, True)

        def stageB1(ti):
            rows = slice(ti * 128, (ti + 1) * 128)
            isctx = ti < 2
            X = Xs[ti % 2]; qf = qfs[ti % 2]
            h0 = 8 if isctx else 0
            if not isctx:
                for half in range(2):
                    for k in range(8):
                        P.pe(lambda e, k=k, half=half: e.matmul(PS[1 + half][:], lhsT=X[:, k, :], rhs=w_in[:, k, half * 512:(half + 1) * 512], start=(k == 0), stop=(k == 7)),
                             r=[X, w_in], w=[PS[1 + half]])
                    P.act(lambda e, half=half: e.activation(out=qf[:, half * 4:(half + 1) * 4, :], in_=PS[1 + half][:].rearrange("p (h d) -> p h d", h=4), func=AF.Copy),
                          r=[PS[1 + half]], w=[qf])
            for k in range(8):
                P.pe(lambda e, k=k: e.matmul(PS[3][:], lhsT=X[:, k, :], rhs=w_in[:, k, 1024:1536], start=(k == 0), stop=(k == 7)), r=[X, w_in], w=[PS[3]])
            P.act(lambda e: e.activation(out=qf[:, 8:10, :], in_=PS[3][:, 0:256].rearrange("p (h d) -> p h d", h=2), func=AF.Copy), r=[PS[3]], w=[qf])
            vv = v1[ti % 2]
            P.act(lambda e, vv=vv: e.activation(out=vv[:], in_=PS[3][:, 256:512], func=AF.Copy), r=[PS[3]], w=[vv])
            P.dma('sp', S['V1'][rows, :, :].rearrange("r h d -> r (h d)"), vv[:], vv, False)

        def stageB2(ti):
            rows = slice(ti * 128, (ti + 1) * 128)
            isctx = ti < 2
            tb = tab[ti % 3]; qf = qfs[ti % 2]
            h0 = 8 if isctx else 0
            nh = 10 - h0
            P.dve(lambda e, h0=h0: e.tensor_tensor(out=sq[:, h0:10, :], in0=qf[:, h0:10, :], in1=qf[:, h0:10, :], op=ALU.mult), r=[qf], w=[sq])
            P.dve(lambda e, h0=h0: e.tensor_reduce(out=ssq[:, h0:10], in_=sq[:, h0:10, :], axis=AX.X, op=ALU.add), r=[sq], w=[ssq])
            P.act(lambda e, h0=h0: e.activation(out=ssq[:, h0:10], in_=ssq[:, h0:10], func=AF.Sqrt, scale=1.0 / 128, bias=EPS), r=[ssq], w=[ssq])
            P.dve(lambda e, h0=h0: e.reciprocal(out=rr[:, h0:10], in_=ssq[:, h0:10]), r=[ssq], w=[rr])
            P.dve(lambda e, h0=h0, nh=nh: e.tensor_tensor(out=qf[:, h0:10, :], in0=qf[:, h0:10, :], in1=rr[:, h0:10].unsqueeze(2).to_broadcast([128, nh, 128]), op=ALU.mult),
                  r=[qf, rr], w=[qf])
            if not isctx:
                P.dve(lambda e: e.tensor_tensor(out=qf[:, 0:8, :], in0=qf[:, 0:8, :], in1=qn[:].unsqueeze(1).to_broadcast([128, 8, 128]), op=ALU.mult), r=[qf, qn], w=[qf])
            P.dve(lambda e: e.tensor_tensor(out=qf[:, 8:10, :], in0=qf[:, 8:10, :], in1=kn[:].unsqueeze(1).to_broadcast([128, 2, 128]), op=ALU.mult), r=[qf, kn], w=[qf])
            rope_tm(C, qf, qf[:, h0:10, :], tb, nh, 128, qk_tm, qk_tm[:, h0:10, :], t1, t2)
            oT = qkT[ti % 2]
            for (a, b_, bank) in ((0, 4, 4), (4, 8, 5), (8, 10, 6)):
                if a < h0:
                    continue
                pbh = PS[bank][:].bitcast(BF16)
                for h in range(a, b_):
                    P.pe(lambda e, h=h, a=a, pbh=pbh: e.transpose(out=pbh[:, (h - a) * 128:(h - a + 1) * 128], in_=qk_tm[:, h, :], identity=C.idb[:]),
                         r=[qk_tm, C.idb], w=[PS[bank]])
                P.dve(lambda e, a=a, b_=b_, pbh=pbh, oT=oT: e.tensor_copy(out=oT[:, a:b_, :], in_=pbh[:, 0:(b_ - a) * 128].rearrange("p (h t) -> p h t", h=b_ - a)),
                      r=[PS[bank]], w=[oT])
            if not isctx:
                P.dma('sp', S['QT1'][:, :, rows].rearrange("h d r -> d h r"), oT[:, 0:8, :], oT, False)
            P.dma('sp', S['KT1'][:, :, rows].rearrange("h d r -> d h r"), oT[:, 8:10, :], oT, False)

        stageA(0)
        for ti in range(nrt):
            if ti + 1 < nrt:
                stageA(ti + 1)
            stageB1(ti)
            if ti >= 1:
                stageB2(ti - 1)
        stageB2(nrt - 1)


def phase_ml_prep(C):
    P, I, S, PS = C.P, C.I, C.S, C.PS
    S['QMLT'] = C.dscr('QMLT', [4, 128, R], BF16)
    S['KMLT'] = C.dscr('KMLT', [4, 128, R], BF16)
    S['KTM'] = C.dscr('KTM', [R, 4, 128], BF16)
    with ExitStack() as ph:
        cw = P.sb(ph, 'c_cw', [128, 8, 3], F32)
        with C.nc.allow_non_contiguous_dma(reason="tiny conv weight transpose"):
            pass
        xin = [P.sb(ph, f'c_xin{i}', [128, 514], F32) for i in range(2)]
        accs_ = [P.sb(ph, f'c_acc{i}', [128, 512], F32) for i in range(2)]
        sls_ = [P.sb(ph, f'c_sl{i}', [128, 512], F32) for i in range(2)]
        ob = [P.sb(ph, f'c_ob{i}', [128, 512], BF16) for i in range(2)]
        ktm = [P.sb(ph, f'c_ktm{i}', [128, 4, 128], BF16) for i in range(2)]
        for j in range(3):
            for cc in range(8):
                pass
        P.dma('sp', cw[:], I['convT'][:, :, :], cw, True)
        it = 0
        segs = [(0, CT, 0, CT)] + [(CT + m * 512, 512, CT, R) for m in range(T // 512)]
        if C.limit_mt is not None:
            segs = segs[:C.limit_mt]
        for (c0, W, lo, hi) in segs:
            for cc in range(8):
                x_ = xin[it % 2]; o_ = ob[it % 2]; acc = accs_[it % 2]; sl = sls_[it % 2]; it += 1
                a = max(c0 - 1, lo); b = min(c0 + W + 1, hi)
                if a > c0 - 1:
                    P.pool(lambda e, x_=x_: e.memset(x_[:, 0:1], 0.0), w=[x_])
                if b < c0 + W + 1:
                    P.pool(lambda e, x_=x_, W=W: e.memset(x_[:, W + 1:W + 2], 0.0), w=[x_])
                P.dma('sp', x_[:, a - (c0 - 1):b - (c0 - 1)], S['QKPRE'][cc * 128:(cc + 1) * 128, a:b], x_, True)
                P.dve(lambda e, x_=x_, W=W, cc=cc, acc=acc: e.tensor_scalar(out=acc[:, 0:W], in0=x_[:, 0:W], scalar1=cw[:, cc, 0:1], scalar2=None, op0=ALU.mult), r=[x_, cw], w=[acc])
                P.dve(lambda e, x_=x_, W=W, cc=cc, acc=acc: e.scalar_tensor_tensor(out=acc[:, 0:W], in0=x_[:, 1:W + 1], scalar=cw[:, cc, 1:2], in1=acc[:, 0:W], op0=ALU.mult, op1=ALU.add),
                      r=[x_, cw, acc], w=[acc])
                P.dve(lambda e, x_=x_, W=W, cc=cc, acc=acc: e.scalar_tensor_tensor(out=acc[:, 0:W], in0=x_[:, 2:W + 2], scalar=cw[:, cc, 2:3], in1=acc[:, 0:W], op0=ALU.mult, op1=ALU.add),
                      r=[x_, cw, acc], w=[acc])
                if cc < 4:
                    P.act(lambda e, W=W, acc=acc, sl=sl: e.activation(out=sl[:, 0:W], in_=acc[:, 0:W], func=AF.Silu), r=[acc], w=[sl])
                    P.act(lambda e, o_=o_, W=W, sl=sl: e.activation(out=o_[:, 0:W], in_=sl[:, 0:W], func=AF.Copy, scale=float(128 ** -0.5)), r=[sl], w=[o_])
                    P.dma('sp', S['QMLT'][cc, :, c0:c0 + W], o_[:, 0:W], o_, False)
                else:
                    P.act(lambda e, o_=o_, W=W, acc=acc: e.activation(out=o_[:, 0:W], in_=acc[:, 0:W], func=AF.Silu), r=[acc], w=[o_])
                    P.dma('sp', S['KMLT'][cc - 4, :, c0:c0 + W], o_[:, 0:W], o_, False)
                    pb = PS[(cc % 2)][:].bitcast(BF16)
                    for r in range(W // 128):
                        P.pe(lambda e, r=r, o_=o_, pb=pb: e.transpose(out=pb[:, r * 128:(r + 1) * 128], in_=o_[:, r * 128:(r + 1) * 128], identity=C.idb[:]),
                             r=[o_, C.idb], w=[PS[cc % 2]])
                    kt_ = ktm[cc % 2]
                    nr = W // 128
                    P.dve(lambda e, kt_=kt_, pb=pb, nr=nr: e.tensor_copy(out=kt_[:, 0:nr, :], in_=pb[:, 0:nr * 128].rearrange("p (r d) -> p r d", r=nr)), r=[PS[cc % 2]], w=[kt_])
                    P.dma('sp', S['KTM'][c0:c0 + W, cc - 4, :].rearrange("(r p) d -> p r d", p=128), kt_[:, 0:nr, :], kt_, False)


def phase_ml_scan(C):
    P, I, S, PS = C.P, C.I, C.S, C.PS
    NCH = R // 128
    S['HAB'] = C.dscr('HAB', [2, R, 4, 128], F32)
    S['WST'] = C.dscr('WST', [4, 2 * NCH], F32)
    nch = NCH if C.limit_mt is None else 2 + 4 * (C.limit_mt - 1)
    orderA = list(range(nch))
    orderB = [1, 0] + list(range(nch - 1, 1, -1))
    with ExitStack() as ph:
        li = P.sb(ph, 's_li', [4, R], F32)
        gf = P.sb(ph, 's_gf', [4, R], F32)
        pp = P.sb(ph, 's_pp', [4, R], F32)
        gb = P.sb(ph, 's_gb', [4, 4], F32); ngb = P.sb(ph, 's_ngb', [4, 4], F32)
        one4 = P.sb(ph, 's_one', [4, 128], F32)
        tot = P.sb(ph, 's_tot', [4, 2, NCH], F32)
        amax = P.sb(ph, 's_amax', [4, 2, NCH], F32)
        Mc = P.sb(ph, 's_Mc', [4, 2, NCH], F32)
        Min = P.sb(ph, 's_Min', [4, 2, NCH], F32)
        mcur = P.sb(ph, 's_mcur', [4, 2], F32)
        wsf = P.sb(ph, 's_wsf', [4, 2, NCH], F32)
        etm = P.sb(ph, 's_etm', [128, 2, 2, NCH, 4], F32)
        wsb = P.sb(ph, 's_wsb', [128, 4, 2, NCH], F32)
        maskA = P.sb(ph, 's_mA', [128, 128], F32); maskB = P.sb(ph, 's_mB', [128, 128], F32)
        G = S['GATES'].rearrange("(d i h) r -> h d i r", d=2, i=2)
        P.dma('sp', gb[:], I['gate_bT'][:, :], gb, True)
        P.dve(lambda e: e.tensor_scalar(out=ngb[:], in0=gb[:], scalar1=-1.0, scalar2=None, op0=ALU.mult), r=[gb], w=[ngb])
        P.pool(lambda e: e.memset(one4[:], 1.0), w=[one4])
        P.pool(lambda e: e.memset(maskA[:], 1.0), w=[maskA])
        P.pool(lambda e: e.affine_select(out=maskA[:], in_=maskA[:], pattern=[[1, 128]], base=0, channel_multiplier=-1, compare_op=ALU.is_ge, fill=0.0),
               r=[maskA], w=[maskA])
        P.pool(lambda e: e.memset(maskB[:], 1.0), w=[maskB])
        P.pool(lambda e: e.affine_select(out=maskB[:], in_=maskB[:], pattern=[[-1, 128]], base=0, channel_multiplier=1, compare_op=ALU.is_ge, fill=0.0),
               r=[maskB], w=[maskB])
        P.pool(lambda e: e.memset(mcur[:], 0.0), w=[mcur])
        P.pool(lambda e: e.memset(Mc[:], 0.0), w=[Mc])
        P.pool(lambda e: e.memset(Min[:], 0.0), w=[Min])
        P.pool(lambda e: e.memset(tot[:], 0.0), w=[tot])
        P.pool(lambda e: e.memset(etm[:], 0.0), w=[etm])
        P3 = lambda t_: t_[:, 0:nch * 128].rearrange("h (c s) -> h c s", s=128)
        NR = nch * 128
        for d, order in ((0, orderA), (1, orderB)):
            P.dma('sp', li[:, 0:NR], G[:, d, 0, 0:NR], li, True)
            P.dma('sp', gf[:, 0:NR], G[:, d, 1, 0:NR], gf, True)
            P.dve(lambda e, d=d: e.tensor_scalar(out=li[:, 0:NR], in0=li[:, 0:NR], scalar1=gb[:, 2 * d:2 * d + 1], scalar2=None, op0=ALU.add), r=[li, gb], w=[li])
            P.act(lambda e, d=d: e.activation(out=gf[:, 0:NR], in_=gf[:, 0:NR], func=AF.Exp, scale=-1.0, bias=ngb[:, 2 * d + 1:2 * d + 2]), r=[gf, ngb], w=[gf])
            P.act(lambda e: e.activation(out=gf[:, 0:NR], in_=gf[:, 0:NR], func=AF.Ln, scale=1.0, bias=1.0), r=[gf], w=[gf])
            for c in range(nch):
                cs = slice(c * 128, (c + 1) * 128)
                P.dve(lambda e, cs=cs: e.tensor_tensor_scan(out=pp[:, cs], data0=one4[:], data1=gf[:, cs], initial=0.0, op0=ALU.mult, op1=ALU.add),
                      r=[one4, gf], w=[pp])
            P.dve(lambda e, d=d: e.tensor_copy(out=tot[:, d, 0:nch], in_=P3(pp)[:, :, 127]), r=[pp], w=[tot])
            if d == 1:
                P.dve(lambda e: e.tensor_tensor(out=pp[:, 0:NR], in0=gf[:, 0:NR], in1=pp[:, 0:NR], op=ALU.subtract), r=[gf, pp], w=[pp])
                P.dve(lambda e: e.tensor_tensor(out=P3(pp), in0=P3(pp), in1=tot[:, 1, 0:nch].unsqueeze(2).to_broadcast([4, nch, 128]), op=ALU.add), r=[pp, tot], w=[pp])
            P.dve(lambda e: e.tensor_tensor(out=li[:, 0:NR], in0=li[:, 0:NR], in1=pp[:, 0:NR], op=ALU.add), r=[li, pp], w=[li])
            P.dve(lambda e, d=d: e.tensor_reduce(out=amax[:, d, 0:nch], in_=P3(li), axis=AX.X, op=ALU.max), r=[li], w=[amax])
            for c in order:
                P.dve(lambda e, d=d, c=c: e.tensor_copy(out=Min[:, d, c:c + 1], in_=mcur[:, d:d + 1]), r=[mcur], w=[Min])
                P.dve(lambda e, d=d, c=c: e.tensor_tensor(out=Mc[:, d, c:c + 1], in0=mcur[:, d:d + 1], in1=amax[:, d, c:c + 1], op=ALU.max), r=[mcur, amax], w=[Mc])
                P.dve(lambda e, d=d, c=c: e.tensor_tensor(out=mcur[:, d:d + 1], in0=Mc[:, d, c:c + 1], in1=tot[:, d, c:c + 1], op=ALU.subtract), r=[Mc, tot], w=[mcur])
            for ai, t_ in enumerate((li, pp)):
                P.dve(lambda e, t_=t_, d=d: e.tensor_tensor(out=P3(t_), in0=P3(t_), in1=Mc[:, d, 0:nch].unsqueeze(2).to_broadcast([4, nch, 128]), op=ALU.subtract),
                      r=[t_, Mc], w=[t_])
                P.act(lambda e, t_=t_: e.activation(out=t_[:, 0:NR], in_=t_[:, 0:NR], func=AF.Exp), r=[t_], w=[t_])
                bank = PS[ai * 2 + d]
                for c in range(nch):
                    P.pe(lambda e, t_=t_, c=c, bank=bank: e.transpose(out=bank[:, c * 4:(c + 1) * 4], in_=t_[:, c * 128:(c + 1) * 128], identity=C.idf[0:4, 0:4]),
                         r=[t_, C.idf], w=[bank])
                P.dve(lambda e, ai=ai, d=d, bank=bank: e.tensor_copy(out=etm[:, ai, d, 0:nch, :], in_=bank[:, 0:nch * 4].rearrange("p (c h) -> p c h", h=4)),
                      r=[bank], w=[etm])
        P.dve(lambda e: e.tensor_tensor(out=wsf[:], in0=Min[:], in1=Mc[:], op=ALU.subtract), r=[Min, Mc], w=[wsf])
        P.act(lambda e: e.activation(out=wsf[:], in_=wsf[:], func=AF.Exp), r=[wsf], w=[wsf])
        P.dma('sp', S['WST'][:, :], wsf[:].rearrange("h d c -> h (d c)"), wsf, False)
        P.barrier()
        P.dma('sp', wsb[:], S['WST'].rearrange("h x -> (h x)").partition_broadcast(128), wsb, True)
        NB = 3
        qT = [P.sb(ph, f's_qT{i}', [128, 4, 128], BF16) for i in range(NB)]
        kT = [P.sb(ph, f's_kT{i}', [128, 4, 128], BF16) for i in range(NB)]
        kM = [P.sb(ph, f's_kM{i}', [128, 4, 128], BF16) for i in range(NB)]
        vM = [P.sb(ph, f's_vM{i}', [128, 4, 128], BF16) for i in range(NB)]
        hO = [P.sb(ph, f's_hO{i}', [128, 4, 128], F32) for i in range(NB)]
        Cst = [[P.sb(ph, f's_C{d}{h}', [128, 129], F32) for h in range(4)] for d in range(2)]
        Cbf = [[P.sb(ph, f's_Cb{d}{h}', [128, 129], BF16) for h in range(4)] for d in range(2)]
        Sm = [P.sb(ph, f's_Sm{i}', [128, 128], BF16) for i in range(4)]
        vE = [P.sb(ph, f's_vE{i}', [128, 129], BF16) for i in range(4)]
        dn = [P.sb(ph, f's_dn{i}', [128, 1], F32) for i in range(4)]
        for d in range(2):
            for h in range(4):
                P.pool(lambda e, d=d, h=h: e.memset(Cst[d][h][:], 0.0), w=[Cst[d][h]])
        bi = 0; ui = 0
        pend = []
        bNs = (PS[6], PS[0], PS[2]); bCs = (PS[7], PS[1], PS[3])
        for step in range(nch):
            for d, order, mask in ((0, orderA, maskA), (1, orderB, maskB)):
                c = order[step]
                rows = slice(c * 128, (c + 1) * 128)
                b_ = bi % NB; bi += 1
                P.dma('sp', qT[b_][:], S['QMLT'][:, :, rows].rearrange("h d r -> d h r"), qT[b_], True)
                P.dma('sp', kT[b_][:], S['KMLT'][:, :, rows].rearrange("h d r -> d h r"), kT[b_], True)
                P.dma('sp', kM[b_][:], S['KTM'][rows, :, :], kM[b_], True)
                P.dma('sp', vM[b_][:], S['VML'][rows, :].rearrange("r (h d) -> r h d", h=4), vM[b_], True)
                for h in range(4):
                    u = ui % 4; u3 = ui % 3; ui += 1
                    Cs = Cst[d][h]; Cb = Cbf[d][h]
                    wsc = wsb[:, h, d, c:c + 1]
                    bS = PS[4 + (u % 2)]; bN = bNs[u3]; bC = bCs[u3]
                    P.dve(lambda e, Cs=Cs, Cb=Cb, wsc=wsc: e.tensor_scalar(out=Cb[:], in0=Cs[:], scalar1=wsc, scalar2=None, op0=ALU.mult), r=[Cs, wsb], w=[Cb])
                    P.pe(lambda e, b_=b_, h=h, bS=bS: e.matmul(bS[:, 0:128], lhsT=kT[b_][:, h, :], rhs=qT[b_][:, h, :], start=True, stop=True), r=[kT[b_], qT[b_]], w=[bS])
                    P.dve(lambda e, u=u, bS=bS, mask=mask: e.tensor_tensor(out=Sm[u][:], in0=bS[:, 0:128], in1=mask[:], op=ALU.mult), r=[bS, mask], w=[Sm[u]])
                    P.dve(lambda e, u=u, b_=b_, h=h, d=d, c=c: e.tensor_scalar(out=vE[u][:, 0:128], in0=vM[b_][:, h, :], scalar1=etm[:, 0, d, c, h:h + 1], scalar2=None, op0=ALU.mult),
                          r=[vM[b_], etm], w=[vE[u]])
                    P.act(lambda e, u=u, d=d, c=c, h=h: e.activation(out=vE[u][:, 128:129], in_=etm[:, 0, d, c, h:h + 1], func=AF.Copy), r=[etm], w=[vE[u]])
                    P.pe(lambda e, u=u, bN=bN: e.matmul(bN[:, 0:129], lhsT=Sm[u][:], rhs=vE[u][:], start=True, stop=False), r=[Sm[u], vE[u]], w=[bN])
                    P.pe(lambda e, b_=b_, h=h, Cb=Cb, bN=bN: e.matmul(bN[:, 0:129], lhsT=qT[b_][:, h, :], rhs=Cb[:], start=False, stop=True), r=[qT[b_], Cb], w=[bN])
                    P.pe(lambda e, u=u, b_=b_, h=h, bC=bC: e.matmul(bC[:, 0:129], lhsT=kM[b_][:, h, :], rhs=vE[u][:], start=True, stop=True), r=[kM[b_], vE[u]], w=[bC])

                    def tail(u=u, h=h, d=d, c=c, b_=b_, Cs=Cs, wsc=wsc, bN=bN, bC=bC, rows=rows):
                        P.dve(lambda e: e.scalar_tensor_tensor(out=Cs[:], in0=Cs[:], scalar=wsc, in1=bC[:, 0:129], op0=ALU.mult, op1=ALU.add),
                              r=[Cs, wsb, bC], w=[Cs])
                        P.act(lambda e: e.activation(out=dn[u][:], in_=bN[:, 128:129], func=AF.Abs), r=[bN], w=[dn[u]])
                        P.dve(lambda e: e.tensor_tensor(out=dn[u][:], in0=dn[u][:], in1=etm[:, 1, d, c, h:h + 1], op=ALU.max),
                              r=[dn[u], etm], w=[dn[u]])
                        P.dve(lambda e: e.reciprocal(out=dn[u][:], in_=dn[u][:]), r=[dn[u]], w=[dn[u]])
                        P.act(lambda e: e.activation(out=hO[b_][:, h, :], in_=bN[:, 0:128], func=AF.Copy, scale=dn[u][:, 0:1]), r=[bN, dn[u]], w=[hO[b_]])
                        if h == 3:
                            P.dma('sp', S['HAB'][d, rows, :, :], hO[b_][:], hO[b_], False)
                    pend.append(tail)
                    if len(pend) > 2:
                        pend.pop(0)()
        while pend:
            pend.pop(0)()


def phase_ml_out(C, MIXT):
    P, I, S, PS = C.P, C.I, C.S, C.PS
    nch = R // 128 if C.limit_mt is None else 2 + 4 * (C.limit_mt - 1)
    with ExitStack() as ph:
        hn = load_bcast(C, ph, 'o_hn', I['head_norm'], 512)
        ha = [P.sb(ph, f'o_ha{i}', [128, 4, 128], F32) for i in range(2)]
        hb = [P.sb(ph, f'o_hb{i}', [128, 4, 128], F32) for i in range(2)]
        og = [P.sb(ph, f'o_og{i}', [128, 512], F32) for i in range(2)]
        sq = P.sb(ph, 'o_sq', [128, 4, 128], F32)
        ssq = P.sb(ph, 'o_ssq', [128, 4], F32); rr = P.sb(ph, 'o_rr', [128, 4], F32)
        mb = P.sb(ph, 'o_mb', [128, 4, 128], BF16)
        mT = [P.sb(ph, f'o_mT{i}', [128, 4, 128], BF16) for i in range(2)]
        for c in range(nch):
            rows = slice(c * 128, (c + 1) * 128)
            a = ha[c % 2]; b = hb[c % 2]; o = og[c % 2]
            P.dma('sp', a[:], S['HAB'][0, rows, :, :], a, True)
            P.dma('sp', b[:], S['HAB'][1, rows, :, :], b, True)
            P.dma('sp', o[:], S['OSIG'][rows, :], o, True)
            P.dve(lambda e, a=a, b=b: e.tensor_tensor(out=a[:], in0=a[:], in1=b[:], op=ALU.add), r=[a, b], w=[a])
            P.dve(lambda e, a=a: e.tensor_tensor(out=sq[:], in0=a[:], in1=a[:], op=ALU.mult), r=[a], w=[sq])
            P.dve(lambda e: e.tensor_reduce(out=ssq[:], in_=sq[:], axis=AX.X, op=ALU.add), r=[sq], w=[ssq])
            P.act(lambda e: e.activation(out=ssq[:], in_=ssq[:], func=AF.Sqrt, scale=1.0 / 128, bias=EPS), r=[ssq], w=[ssq])
            P.dve(lambda e: e.reciprocal(out=rr[:], in_=ssq[:]), r=[ssq], w=[rr])
            P.dve(lambda e, a=a: e.tensor_tensor(out=a[:], in0=a[:], in1=rr[:].unsqueeze(2).to_broadcast([128, 4, 128]), op=ALU.mult), r=[a, rr], w=[a])
            P.dve(lambda e, o=o: e.tensor_tensor(out=o[:], in0=o[:], in1=hn[:], op=ALU.mult), r=[o, hn], w=[o])
            P.dve(lambda e, a=a, o=o: e.tensor_tensor(out=mb[:], in0=a[:], in1=o[:].rearrange("p (h d) -> p h d", h=4), op=ALU.mult), r=[a, o], w=[mb])
            pb = PS[c % 2][:].bitcast(BF16)
            for h in range(4):
                P.pe(lambda e, h=h, pb=pb: e.transpose(out=pb[:, h * 128:(h + 1) * 128], in_=mb[:, h, :], identity=C.idb[:]), r=[mb, C.idb], w=[PS[c % 2]])
            m_ = mT[c % 2]
            P.act(lambda e, m_=m_, pb=pb: e.activation(out=m_[:], in_=pb[:, 0:512].rearrange("p (h t) -> p h t", h=4), func=AF.Copy), r=[PS[c % 2]], w=[m_])
            P.dma('sp', MIXT[512:1024, rows].rearrange("(h d) r -> d h r", h=4), m_[:], m_, False)


def rope_tables():
    def ang(drot):
        nf = drot // 4
        freqs = (10000.0 ** (-np.arange(nf, dtype=np.float32) / nf)).astype(np.float32)
        t = np.arange(T)
        pos = np.stack([t // 64, t % 64], -1).astype(np.float32)
        return pos[:, :, None] * freqs

    out = {}
    for name, drot in (('ropeM', 32), ('ropeG', 128)):
        a = ang(drot)
        nf = drot // 4
        cos = np.cos(a)[:, :, None, :].repeat(2, axis=2)
        sin = np.sin(a)
        ssin = np.stack([-sin, sin], axis=2)
        tab = np.zeros((R, 2, drot), np.float32)
        tab[:CT, 0, :] = 1.0
        tab[CT:, 0, :] = cos.reshape(T, drot)
        tab[CT:, 1, :] = ssin.reshape(T, drot)
        out[name] = tab
    return out


def make_in_maps(inp):
    f = lambda a: np.ascontiguousarray(np.asarray(a, dtype=np.float32))
    rt = rope_tables()
    w_uq = f(inp['l0_mla_w_uq']).reshape(256, 8, 96)
    w_uq = np.concatenate([w_uq[:, :, :64].reshape(256, 512), w_uq[:, :, 64:].reshape(256, 256)], axis=1)
    w_ukv = f(inp['l0_mla_w_ukv']).reshape(128, 8, 128)
    w_ukv = np.concatenate([w_ukv[:, :, :64].reshape(128, 512), w_ukv[:, :, 64:].reshape(128, 512)], axis=1)
    shared = {
        'w_in0': f(inp['l0_w_in']), 'w_in1': f(inp['l1_w_in']),
        'q_norm0': f(inp['l0_mla_q_norm']), 'kv_norm0': f(inp['l0_mla_kv_norm']),
        'w_uq': f(w_uq), 'w_ukv': f(w_ukv), 'convT': f(f(inp['l0_ml_conv']).reshape(3, 8, 128).transpose(2, 1, 0)),
        'gate_bT': f(f(inp['l0_ml_gate_b']).reshape(2, 2, 4).transpose(2, 0, 1).reshape(4, 4)),
        'head_norm': f(inp['l0_ml_head_norm']), 'q_norm1': f(inp['l1_q_norm']), 'k_norm1': f(inp['l1_k_norm']),
        'final_norm': f(inp['final_norm']), 'ropeM': rt['ropeM'], 'ropeG': rt['ropeG'],
    }
    for l in (0, 1):
        shared[f'ada_w{l}'] = f(inp[f'l{l}_ada_w']); shared[f'ada_b{l}'] = f(inp[f'l{l}_ada_b'])
        shared[f'norm1_{l}'] = f(inp[f'l{l}_norm1']); shared[f'norm2_{l}'] = f(inp[f'l{l}_norm2'])
        shared[f'w_out{l}'] = f(inp[f'l{l}_w_out']); shared[f'w1_{l}'] = f(inp[f'l{l}_w1']); shared[f'w2_{l}'] = f(inp[f'l{l}_w2'])
    x = f(inp['x']); ctx = f(inp['ctx']); c = f(inp['c']); c_ctx = f(inp['c_ctx'])
    maps = []
    for cidx in range(NCORES):
        b, par = cidx // 2, cidx % 2
        m = dict(shared)
        m['sel'] = np.ascontiguousarray(np.tile(np.eye(2, dtype=np.float32)[par][None, :], (128, 1)))
        m['rows'] = np.concatenate([ctx[b], x[b]], axis=0)
        cc = np.stack([c[b], c_ctx], axis=-1)
        m['cT'] = np.ascontiguousarray(cc.reshape(8, 128, 2).transpose(1, 0, 2))
        maps.append(m)
    return maps


def kernel(**inputs):
    nc = build()
    maps = make_in_maps(inputs)
    res = run_bass_kernel_spmd(nc, maps, core_ids=list(range(NCORES)))
    out = np.empty((NCORES // 2, T, D), np.float32)
    for cidx, r in enumerate(res.results):
        out[cidx // 2, (cidx % 2) * TH:(cidx % 2 + 1) * TH] = r['y']
    return out
```

```python
import numpy as np
from contextlib import ExitStack
import concourse.bass as bass
import concourse.mybir as mybir
from concourse.bass_utils import run_bass_kernel_spmd

F32 = mybir.dt.float32
BF16 = mybir.dt.bfloat16
AF = mybir.ActivationFunctionType
ALU = mybir.AluOpType
AX = mybir.AxisListType

D = 1024
T = 8192
CT = 256
R = T + CT
NCORES = 8
TH = T // 2
EPS = 1e-6
EPOCH = 30000
HOIST = True
STRICT_SAME_ENGINE = True
DVE_L = True
EPI_DEFER = 4
ENGS = ('pe', 'act', 'dve', 'pool', 'sp')
BNAME = {'pe': 'tensor', 'act': 'scalar', 'dve': 'vector', 'pool': 'gpsimd', 'sp': 'sync'}


class Tl:
    def __init__(s, t, name):
        s.t = t; s.name = name
        s.w = None
        s.r = {}
        s.dsem = None; s.dent = None; s.dgen = -1

    def __getitem__(s, k):
        return s.t[k]


class Prog:
    def __init__(s, nc, st):
        s.nc = nc; s.st = st
        s.ops = {e: [] for e in ENGS}
        s.esem = {}; s.ecnt = {e: 0 for e in ENGS}
        s.waited = {e: {} for e in ENGS}
        s.sems = {}
        s.nsem = 0
        s.last = {}
        s.gen = 0; s.gseq = 0
        s.keys = {e: [] for e in ENGS}; s.floor = {e: 0 for e in ENGS}
        s.dpool = []; s.dfree = []

    def newsem(s, name):
        s.nsem += 1
        h = s.st.enter_context(s.nc.semaphore(f"{name}_{s.nsem}"))
        return h

    def sb(s, st, name, shape, dt):
        t = Tl(st.enter_context(s.nc.sbuf_tensor(name, list(shape), dt)), name)
        t.shape = list(shape)
        return t

    def ps(s, st, name, shape, dt):
        t = Tl(st.enter_context(s.nc.psum_tensor(name, list(shape), dt)), name)
        return t

    def _deps(s, eng, reads, writes, is_dma, dedupe=True):
        deps = []
        for t in reads:
            if t.w is not None:
                deps.append((t.w, 'raw'))
        for t in writes:
            if t.w is not None:
                deps.append((t.w, 'waw'))
            for ev in t.r.values():
                deps.append((ev, 'war'))
        out = {}
        gmax = -1
        for (sem, val, oeng, gen, gs), kind in deps:
            if gen < s.gen:
                continue
            if (not is_dma) and oeng == eng:
                if eng == 'pe' or (kind != 'raw' and not STRICT_SAME_ENGINE):
                    continue
            gmax = max(gmax, gs)
            k = id(sem)
            if dedupe and s.waited[eng].get(k, 0) >= val:
                continue
            if k not in out or out[k][1] < val:
                out[k] = (sem, val)
        if dedupe:
            for k, (sem, val) in out.items():
                s.waited[eng][k] = val
        return list(out.values()), gmax

    def _mark(s, ev, reads, writes):
        for t in writes:
            t.w = ev; t.r = {}
        for t in reads:
            k = id(ev[0])
            t.r[k] = ev
        s.last[id(ev[0])] = (ev[0], ev[1])

    def _append(s, eng, rec, key):
        s.ops[eng].append(rec); s.keys[eng].append(key)

    def op(s, eng, fn, r=(), w=()):
        waits, _ = s._deps(eng, r, w, False)
        c = s.ecnt[eng]
        ep = c // EPOCH
        key = (eng, ep)
        if key not in s.esem:
            s.esem[key] = s.newsem(f"e{eng}{ep}")
        sem = s.esem[key]
        val = c % EPOCH + 1
        s.ecnt[eng] = c + 1
        s.gseq += 1
        ev = (sem, val, eng, s.gen, s.gseq)
        s._append(eng, (waits, fn, (sem, 1)), s.gseq)
        s._mark(ev, r, w)

    def pe(s, fn, r=(), w=()): s.op('pe', fn, r, w)
    def act(s, fn, r=(), w=()): s.op('act', fn, r, w)
    def dve(s, fn, r=(), w=()): s.op('dve', fn, r, w)
    def pool(s, fn, r=(), w=()): s.op('pool', fn, r, w)

    def _grab_dsem(s, tile):
        while s.dfree:
            ent = s.dfree.pop()
            if ent[1] + 4096 < EPOCH:
                break
        else:
            ent = [s.newsem("dq"), 0]
            s.dpool.append(ent)
        tile.dent = ent
        tile.dsem = ent[0]

    def dma(s, q, out, in_, tile, load, extra_r=(), extra_w=(), hoist=True, **kw):
        r = list(extra_r); w = list(extra_w)
        if load:
            w.append(tile)
        else:
            r.append(tile)
        hoist = hoist and load and HOIST
        waits, gmax = s._deps(q, r, w, True, dedupe=not hoist)
        if tile.dsem is None or tile.dgen != s.gen or tile.dent[1] + 16 > EPOCH:
            s._grab_dsem(tile); tile.dgen = s.gen
        tile.dent[1] += 16
        s.gseq += 1
        ev = (tile.dsem, tile.dent[1], None, s.gen, s.gseq)
        rec = (waits, (lambda e, o=out, i=in_, k=kw: e.dma_start(out=o, in_=i, **k)), (tile.dsem, 16))
        if hoist:
            import bisect
            key = gmax + 0.5
            pos = max(bisect.bisect_right(s.keys[q], key), s.floor[q])
            s.ops[q].insert(pos, rec); s.keys[q].insert(pos, key)
            ev = (tile.dsem, tile.dent[1], None, s.gen, key)
        else:
            s._append(q, rec, s.gseq)
        s._mark(ev, r, w)

    def barrier(s):
        allev = list(s.last.values())
        s.gseq += 1
        for eng in ENGS:
            waits = []
            for sem, val in allev:
                k = id(sem)
                if s.waited[eng].get(k, 0) >= val:
                    continue
                s.waited[eng][k] = val
                waits.append((sem, val))
            if waits:
                s._append(eng, (waits, None, None), s.gseq)
            s.floor[eng] = len(s.ops[eng])

    def flush(s):
        with s.nc.Block() as block:
            for eng in ENGS:
                ops = s.ops[eng]
                if not ops:
                    continue

                def body(e, ops=ops):
                    for waits, fn, inc in ops:
                        for sem, val in waits:
                            e.wait_ge(sem, val)
                        if fn is not None:
                            ins = fn(e)
                            if inc is not None:
                                ins.then_inc(inc[0], inc[1])
                getattr(block, BNAME[eng])(body)
        s.ops = {e: [] for e in ENGS}
        s.keys = {e: [] for e in ENGS}; s.floor = {e: 0 for e in ENGS}
        s.gen += 1
        s.dfree = list(s.dpool)
        s.last = {}


def w_in_names():
    return None


class K:
    pass


def build(dbg=None, stop_after=None, limit_mt=None, limit_heads=None, limit_qb=None, skip=None):
    nc = bass.Bass("TRN2", target_bir_lowering=False)
    dbg = dbg or []

    def din(name, shape, dt=F32):
        return nc.dram_tensor(name, list(shape), dt, kind="ExternalInput").ap()

    def dscr(name, shape, dt):
        kind = "ExternalOutput" if name in dbg else "Internal"
        return nc.dram_tensor(name, list(shape), dt, kind=kind).ap()

    I = {}
    I['rows'] = din('rows', [R, D])
    I['cT'] = din('cT', [128, 8, 2])
    for l in (0, 1):
        I[f'ada_w{l}'] = din(f'ada_w{l}', [D, 6 * D])
        I[f'ada_b{l}'] = din(f'ada_b{l}', [6 * D])
        I[f'norm1_{l}'] = din(f'norm1_{l}', [D])
        I[f'norm2_{l}'] = din(f'norm2_{l}', [D])
        I[f'w_out{l}'] = din(f'w_out{l}', [D, D])
        I[f'w1_{l}'] = din(f'w1_{l}', [D, 4 * D])
        I[f'w2_{l}'] = din(f'w2_{l}', [4 * D, D])
    I['w_in0'] = din('w_in0', [D, 2480])
    I['w_in1'] = din('w_in1', [D, 1536])
    I['q_norm0'] = din('q_norm0', [256])
    I['kv_norm0'] = din('kv_norm0', [128])
    I['w_uq'] = din('w_uq', [256, 768])
    I['w_ukv'] = din('w_ukv', [128, 1024])
    I['convT'] = din('convT', [128, 8, 3])
    I['gate_bT'] = din('gate_bT', [4, 4])
    I['head_norm'] = din('head_norm', [512])
    I['q_norm1'] = din('q_norm1', [128])
    I['k_norm1'] = din('k_norm1', [128])
    I['final_norm'] = din('final_norm', [D])
    I['ropeM'] = din('ropeM', [R, 2, 32])
    I['ropeG'] = din('ropeG', [R, 2, 128])
    I['sel'] = din('sel', [128, 2])
    y = nc.dram_tensor("y", [TH, D], F32, kind="ExternalOutput").ap()

    S = {}
    S['MODS'] = dscr('MODS', [2, 2, 6 * D], F32)

    with ExitStack() as st:
        P = Prog(nc, st)
        idb = P.sb(st, 'idb', [128, 128], BF16)
        idf = P.sb(st, 'idf', [128, 128], F32)
        P.pool(lambda e: e.memset(idb[:], 1.0), w=[idb])
        P.pool(lambda e: e.affine_select(out=idb[:], in_=idb[:], pattern=[[-1, 128]], base=0, channel_multiplier=1,
                                         compare_op=ALU.is_equal, fill=0.0), r=[idb], w=[idb])
        P.pool(lambda e: e.memset(idf[:], 1.0), w=[idf])
        P.pool(lambda e: e.affine_select(out=idf[:], in_=idf[:], pattern=[[-1, 128]], base=0, channel_multiplier=1,
                                         compare_op=ALU.is_equal, fill=0.0), r=[idf], w=[idf])
        PS = [P.ps(st, f'ps{i}', [128, 512], F32) for i in range(8)]
        C = K()
        C.nc, C.P, C.I, C.S, C.PS, C.idb, C.idf, C.y, C.dscr = nc, P, I, S, PS, idb, idf, y, dscr
        selt = P.sb(st, 'selt', [128, 2], F32)
        P.dma('sp', selt[:], I['sel'][:, :], selt, True)
        C.sel = selt

        C.limit_mt = limit_mt; C.limit_heads = limit_heads; C.limit_qb = limit_qb
        phase_mods(C)
        P.barrier(); P.flush()
        if stop_after == 'mods':
            return nc
        phase_a0(C)
        P.barrier(); P.flush()
        if stop_after == 'a0':
            return nc
        S['MIXT0'] = dscr('MIXT0', [1024, R], BF16)
        S['SN0'] = dscr('SN0', [R, D], F32)
        S['UT'] = dscr('UT', [4096, R], BF16)
        S['S1'] = dscr('S1', [R, D], F32)
        S['MIXT1'] = dscr('MIXT1', [1024, TH], BF16)
        S['SN1'] = dscr('SN1', [TH, D], F32)
        steps = [
            ('attn0', lambda: phase_attn(C, 'x_', S['QT0'], S['KT0'], S['V0'], 8, 8, 96, 64, float(96 ** -0.5), S['MIXT0'], True, merged=True)),
            ('mlprep', lambda: phase_ml_prep(C)),
            ('mlscan', lambda: phase_ml_scan(C)),
            ('mlout', lambda: phase_ml_out(C, S['MIXT0'])),
            ('f1_0', lambda: phase_f1(C, 'f_', 0, S['MIXT0'], I['rows'], S['SN0'], S['UT'], True)),
            ('f2_0', lambda: phase_f2(C, 'g_', 0, S['SN0'], S['UT'], S['S1'], True, False)),
            ('a1', lambda: phase_a1(C)),
            ('attn1', lambda: phase_attn(C, 'y_', S['QT1'], S['KT1'], S['V1'], 8, 2, 128, 128, float(128 ** -0.5), S['MIXT1'], False, split=True)),
            ('f1_1', lambda: phase_f1(C, 'h_', 1, S['MIXT1'], S['S1'], S['SN1'], S['UT'], False, split=True)),
            ('f2_1', lambda: phase_f2(C, 'i_', 1, S['SN1'], S['UT'], None, False, True, split=True)),
        ]
        for name, fn in steps:
            if skip and name in skip:
                continue
            fn()
            P.barrier(); P.flush()
            if stop_after == name:
                return nc
        P.barrier(); P.flush()
    return nc


def phase_mods(C):
    P, I, S, PS = C.P, C.I, C.S, C.PS
    with ExitStack() as ph:
        cT = P.sb(ph, 'm_cT', [128, 8, 2], F32)
        sc = P.sb(ph, 'm_sc', [128, 8, 2], BF16)
        wst = [P.sb(ph, f'm_wst{i}', [128, 3072], F32) for i in range(4)]
        wbf = [P.sb(ph, f'm_wbf{i}', [128, 3072], BF16) for i in range(4)]
        ab = P.sb(ph, 'm_ab', [2, 6 * D], F32)
        mods = P.sb(ph, 'm_mods', [2, 6 * D], F32)
        nrm = P.sb(ph, 'm_nrm', [2, 2, D], F32)
        P.dma('sp', cT[:], I['cT'][:, :, :], cT, True)
        P.act(lambda e: e.activation(out=sc[:], in_=cT[:], func=AF.Silu), r=[cT], w=[sc])
        it = 0
        for l in (0, 1):
            P.dma('sp', ab[:], I[f'ada_b{l}'].partition_broadcast(2), ab, True)
            P.dma('sp', nrm[:, 0, :], I[f'norm1_{l}'].partition_broadcast(2), nrm, True)
            P.dma('sp', nrm[:, 1, :], I[f'norm2_{l}'].partition_broadcast(2), nrm, True)
            for half in range(2):
                for k in range(8):
                    b = it % 4; it += 1
                    P.dma('sp', wst[b][:], I[f'ada_w{l}'][k * 128:(k + 1) * 128, half * 3072:(half + 1) * 3072], wst[b], True)
                    if it % 2 == 0:
                        P.act(lambda e, b=b: e.activation(out=wbf[b][:], in_=wst[b][:], func=AF.Copy), r=[wst[b]], w=[wbf[b]])
                    else:
                        P.dve(lambda e, b=b: e.tensor_copy(out=wbf[b][:], in_=wst[b][:]), r=[wst[b]], w=[wbf[b]])
                    for j in range(6):
                        P.pe(lambda e, b=b, j=j, k=k: e.matmul(PS[j][0:2, :], lhsT=sc[:, k, :], rhs=wbf[b][:, j * 512:(j + 1) * 512],
                                                               start=(k == 0), stop=(k == 7)), r=[sc, wbf[b]], w=[PS[j]])
                for j in range(6):
                    c0 = half * 3072 + j * 512
                    P.dve(lambda e, j=j, c0=c0: e.tensor_tensor(out=mods[:, c0:c0 + 512], in0=PS[j][0:2, :], in1=ab[:, c0:c0 + 512], op=ALU.add),
                          r=[PS[j], ab], w=[mods])
            for (cs, ni) in ((1, 0), (4, 1)):
                P.dve(lambda e, cs=cs, ni=ni: e.scalar_tensor_tensor(out=mods[:, cs * D:(cs + 1) * D], in0=mods[:, cs * D:(cs + 1) * D], scalar=1.0,
                                                                     in1=nrm[:, ni, :], op0=ALU.add, op1=ALU.mult), r=[mods, nrm], w=[mods])
            P.dma('sp', S['MODS'][l, :, :], mods[:], mods, False)


def load_w(C, ph, name, src, kc, ncol, stg, cast_eng='pool'):
    P = C.P
    w = P.sb(ph, name, [128, kc, ncol], BF16)
    sw = stg[0].shape[1]
    for k in range(kc):
        for c0 in range(0, ncol, sw):
            cw = min(sw, ncol - c0)
            s_ = stg[C.stg_i % len(stg)]; C.stg_i += 1
            P.dma('sp', s_[:, 0:cw], src[k * 128:(k + 1) * 128, c0:c0 + cw], s_, True)
            if C.stg_i % 2 == 0:
                P.act(lambda e, k=k, s_=s_, c0=c0, cw=cw: e.activation(out=w[:, k, c0:c0 + cw], in_=s_[:, 0:cw], func=AF.Copy), r=[s_], w=[w])
            else:
                P.dve(lambda e, k=k, s_=s_, c0=c0, cw=cw: e.tensor_copy(out=w[:, k, c0:c0 + cw], in_=s_[:, 0:cw]), r=[s_], w=[w])
    return w


def load_bcast(C, ph, name, src_ap, n, parts=128):
    t = C.P.sb(ph, name, [parts, n], F32)
    C.P.dma('sp', t[:], src_ap.partition_broadcast(parts), t, True)
    return t


def rstd_of(C, src_t, src_ap, n, junk, ss, out_rs):
    P = C.P
    P.act(lambda e: e.activation(out=junk[:, 0:n], in_=src_ap, func=AF.Square, accum_out=ss[:]), r=[src_t], w=[junk, ss])
    P.act(lambda e: e.activation(out=ss[:], in_=ss[:], func=AF.Sqrt, scale=1.0 / n, bias=EPS), r=[ss], w=[ss])
    P.dve(lambda e: e.reciprocal(out=out_rs[:], in_=ss[:]), r=[ss], w=[out_rs])


def rope_tm(C, src_t, src_ap, tab, nh, dr, dst_t, dst_ap, t1, t2):
    P = C.P
    nf = dr // 4
    cosb = tab[:, 0, :].unsqueeze(1).to_broadcast([128, nh, dr])
    v5 = lambda ap: ap.rearrange("p h (a b f) -> p h a b f", a=2, b=2)
    sin5 = tab[:, 1, :].rearrange("p (a b f) -> p a b f", a=2, b=2)
    a1 = t1[:, 0:nh * dr].rearrange("p (h d) -> p h d", h=nh)
    a2 = t2[:, 0:nh * dr].rearrange("p (h d) -> p h d", h=nh)
    P.dve(lambda e: e.tensor_tensor(out=a1, in0=src_ap, in1=cosb, op=ALU.mult), r=[src_t, tab], w=[t1])
    for b in range(2):
        sb_ = sin5[:, :, b, :].unsqueeze(1).to_broadcast([128, nh, 2, nf])
        P.dve(lambda e, b=b, sb_=sb_: e.tensor_tensor(out=v5(a2)[:, :, :, b, :], in0=v5(src_ap)[:, :, :, 1 - b, :], in1=sb_, op=ALU.mult),
              r=[src_t, tab], w=[t2])
    P.dve(lambda e: e.tensor_tensor(out=dst_ap, in0=a1, in1=a2, op=ALU.add), r=[t1, t2], w=[dst_t])


def norm_mod_rows(C, B, i, src_rows_ap, gm, sh, xmT, col0, track=None):
    P, PS = C.P, C.PS
    xin = B['xin'][i % 2]; xm = B['xm'][i % 2]
    P.dma('sp', xin[:], src_rows_ap, xin, True)
    rstd_of(C, xin, xin[:], D, B['junk'], B['ss'], B['rs'])
    P.dve(lambda e: e.scalar_tensor_tensor(out=B['tmp'][:], in0=xin[:], scalar=B['rs'][:, 0:1], in1=gm[:], op0=ALU.mult, op1=ALU.mult),
          r=[xin, B['rs'], gm], w=[B['tmp']])
    P.dve(lambda e: e.tensor_tensor(out=xm[:], in0=B['tmp'][:], in1=sh[:], op=ALU.add), r=[B['tmp'], sh], w=[xm])
    pb = PS[0][:].bitcast(BF16)
    for k in range(8):
        P.pe(lambda e, k=k: e.transpose(out=pb[:, k * 128:(k + 1) * 128], in_=xm[:, k * 128:(k + 1) * 128], identity=C.idb[:]),
             r=[xm, C.idb], w=[PS[0]])
    P.act(lambda e: e.activation(out=xmT[:, :, col0:col0 + 128], in_=pb.rearrange("p (k t) -> p k t", k=8), func=AF.Copy),
          r=[PS[0]], w=[track if track is not None else xmT])


def norm_bufs(C, ph, pfx):
    P = C.P
    return {'xin': [P.sb(ph, f'{pfx}xin{i}', [128, D], F32) for i in range(2)],
            'xm': [P.sb(ph, f'{pfx}xm{i}', [128, D], BF16) for i in range(2)],
            'tmp': P.sb(ph, f'{pfx}tmp', [128, D], F32), 'junk': P.sb(ph, f'{pfx}junk', [128, D], BF16),
            'ss': P.sb(ph, f'{pfx}ss', [128, 1], F32), 'rs': P.sb(ph, f'{pfx}rs', [128, 1], F32)}


def mtiles():
    out = [(0, CT, True)]
    for m in range(T // 512):
        out.append((CT + m * 512, 512, False))
    return out


def phase_a0(C):
    P, I, S, PS = C.P, C.I, C.S, C.PS
    S['QKPRE'] = C.dscr('QKPRE', [1024, R], F32)
    S['GATES'] = C.dscr('GATES', [16, R], F32)
    S['VML'] = C.dscr('VML', [R, 512], BF16)
    S['OSIG'] = C.dscr('OSIG', [R, 512], F32)
    S['QT0'] = C.dscr('QT0', [8, 96, R], BF16)
    S['KT0'] = C.dscr('KT0', [8, 96, R], BF16)
    S['V0'] = C.dscr('V0', [R, 8, 128], BF16)
    with ExitStack() as ph:
        stg = [P.sb(ph, f'a_stg{i}', [128, 2480], F32) for i in range(4)]
        C.stg_i = 0
        w_in = load_w(C, ph, 'a_win', I['w_in0'], 8, 2480, stg)
        w_uq = load_w(C, ph, 'a_wuq', I['w_uq'], 2, 768, stg)
        w_ukv = load_w(C, ph, 'a_wukv', I['w_ukv'], 1, 1024, stg)
        gm = [load_bcast(C, ph, f'a_gm{j}', S['MODS'][0, j, 1 * D:2 * D], D) for j in range(2)]
        sh = [load_bcast(C, ph, f'a_sh{j}', S['MODS'][0, j, 0 * D:1 * D], D) for j in range(2)]
        qn = load_bcast(C, ph, 'a_qn', I['q_norm0'], 256)
        kvn = load_bcast(C, ph, 'a_kvn', I['kv_norm0'], 128)
        B = norm_bufs(C, ph, 'a_')
        xmT = [P.sb(ph, f'a_xmT{i}', [128, 8, 512], BF16) for i in range(2)]
        lat = P.sb(ph, 'a_lat', [128, 416], F32)
        latn = P.sb(ph, 'a_latn', [128, 384], BF16)
        latT = P.sb(ph, 'a_latT', [128, 3, 128], BF16)
        rq = P.sb(ph, 'a_rq', [128, 1], F32); rkv = P.sb(ph, 'a_rkv', [128, 1], F32)
        tab = [P.sb(ph, f'a_tab{i}', [128, 2, 32], F32) for i in range(2)]
        t1 = P.sb(ph, 'a_t1', [128, 256], F32); t2 = P.sb(ph, 'a_t2', [128, 256], F32)
        kper = P.sb(ph, 'a_kper', [128, 32], BF16)
        q_tm = P.sb(ph, 'a_qtm', [128, 8, 96], BF16); k_tm = P.sb(ph, 'a_ktm', [128, 8, 96], BF16)
        v0 = [P.sb(ph, f'a_v0{i}', [128, 8, 128], BF16) for i in range(2)]
        vml = [P.sb(ph, f'a_vml{i}', [128, 512], BF16) for i in range(2)]
        osg = [P.sb(ph, f'a_osg{i}', [128, 512], F32) for i in range(2)]
        qT = [P.sb(ph, f'a_qT{i}', [96, 8, 128], BF16) for i in range(2)]
        kT = [P.sb(ph, f'a_kT{i}', [96, 8, 128], BF16) for i in range(2)]
        fm = [P.sb(ph, f'a_fm{i}', [128, 512], F32) for i in range(2)]
        gsb = [P.sb(ph, f'a_g{i}', [16, 512], F32) for i in range(2)]
        for i in range(2):
            P.pool(lambda e, i=i: e.memset(v0[i][:], 1.0), w=[v0[i]])
        mts = mtiles()
        if C.limit_mt is not None:
            mts = mts[:C.limit_mt]
        tiles = [(mi, r) for mi, (row0, W, isctx) in enumerate(mts) for r in range(W // 128)]
        xv = [[Tl(xmT[i].t, f'a_xv{i}{r}') for r in range(4)] for i in range(2)]

        def stageA(ti):
            mi, r = tiles[ti]; row0, W, isctx = mts[mi]; j = 1 if isctx else 0
            rows = slice(row0 + r * 128, row0 + (r + 1) * 128)
            norm_mod_rows(C, B, ti, I['rows'][rows, :], gm[j], sh[j], xmT[mi % 2], r * 128, track=xv[mi % 2][r])
            tb = tab[ti % 2]
            P.dma('sp', tb[:], I['ropeM'][rows, :, :], tb, True)

        def stageB(ti):
            mi, r = tiles[ti]; row0, W, isctx = mts[mi]; j = 1 if isctx else 0
            rows = slice(row0 + r * 128, row0 + (r + 1) * 128)
            X = xmT[mi % 2]; XV = xv[mi % 2][r]; tb = tab[ti % 2]
            lhs = lambda k: X[:, k, r * 128:(r + 1) * 128]
            for k in range(8):
                P.pe(lambda e, k=k, l_=lhs(k): e.matmul(PS[1][:, 0:416], lhsT=l_, rhs=w_in[:, k, 0:416], start=(k == 0), stop=(k == 7)),
                     r=[XV, w_in], w=[PS[1]])
            P.dve(lambda e: e.tensor_copy(out=lat[:], in_=PS[1][:, 0:416]), r=[PS[1]], w=[lat])
            for k in range(8):
                P.pe(lambda e, k=k, l_=lhs(k): e.matmul(PS[2][:], lhsT=l_, rhs=w_in[:, k, 1440:1952], start=(k == 0), stop=(k == 7)),
                     r=[XV, w_in], w=[PS[2]])
            vm = vml[ti % 2]
            P.act(lambda e, vm=vm: e.activation(out=vm[:], in_=PS[2][:], func=AF.Copy), r=[PS[2]], w=[vm])
            P.dma('sp', S['VML'][rows, :], vm[:], vm, False)
            for k in range(8):
                P.pe(lambda e, k=k, l_=lhs(k): e.matmul(PS[3][:], lhsT=l_, rhs=w_in[:, k, 1952:2464], start=(k == 0), stop=(k == 7)),
                     r=[XV, w_in], w=[PS[3]])
            og = osg[ti % 2]
            P.act(lambda e, og=og: e.activation(out=og[:], in_=PS[3][:], func=AF.Sigmoid), r=[PS[3]], w=[og])
            P.dma('sp', S['OSIG'][rows, :], og[:], og, False)
            rstd_of(C, lat, lat[:, 0:256], 256, B['junk'], B['ss'], rq)
            rstd_of(C, lat, lat[:, 256:384], 128, B['junk'], B['ss'], rkv)
            P.dve(lambda e: e.scalar_tensor_tensor(out=latn[:, 0:256], in0=lat[:, 0:256], scalar=rq[:, 0:1], in1=qn[:], op0=ALU.mult, op1=ALU.mult),
                  r=[lat, rq, qn], w=[latn])
            P.dve(lambda e: e.scalar_tensor_tensor(out=latn[:, 256:384], in0=lat[:, 256:384], scalar=rkv[:, 0:1], in1=kvn[:], op0=ALU.mult, op1=ALU.mult),
                  r=[lat, rkv, kvn], w=[latn])
            pb = PS[4][:].bitcast(BF16)
            for k in range(3):
                P.pe(lambda e, k=k: e.transpose(out=pb[:, k * 128:(k + 1) * 128], in_=latn[:, k * 128:(k + 1) * 128], identity=C.idb[:]),
                     r=[latn, C.idb], w=[PS[4]])
            P.act(lambda e: e.activation(out=latT[:], in_=pb[:, 0:384].rearrange("p (k t) -> p k t", k=3), func=AF.Copy), r=[PS[4]], w=[latT])
            for kc in range(2):
                P.pe(lambda e, kc=kc: e.matmul(PS[5][:], lhsT=latT[:, kc, :], rhs=w_uq[:, kc, 0:512], start=(kc == 0), stop=(kc == 1)),
                     r=[latT, w_uq], w=[PS[5]])
            for kc in range(2):
                P.pe(lambda e, kc=kc: e.matmul(PS[6][:, 0:256], lhsT=latT[:, kc, :], rhs=w_uq[:, kc, 512:768], start=(kc == 0), stop=(kc == 1)),
                     r=[latT, w_uq], w=[PS[6]])
            P.pe(lambda e: e.matmul(PS[7][:], lhsT=latT[:, 2, :], rhs=w_ukv[:, 0, 0:512], start=True, stop=True), r=[latT, w_ukv], w=[PS[7]])
            P.pe(lambda e: e.matmul(PS[2][:], lhsT=latT[:, 2, :], rhs=w_ukv[:, 0, 512:1024], start=True, stop=True), r=[latT, w_ukv], w=[PS[2]])
            P.act(lambda e: e.activation(out=q_tm[:, :, 0:64], in_=PS[5][:].rearrange("p (h d) -> p h d", h=8), func=AF.Copy), r=[PS[5]], w=[q_tm])
            rope_tm(C, PS[6], PS[6][:, 0:256].rearrange("p (h d) -> p h d", h=8), tb, 8, 32, q_tm, q_tm[:, :, 64:96], t1, t2)
            P.act(lambda e: e.activation(out=k_tm[:, :, 0:64], in_=PS[7][:].rearrange("p (h d) -> p h d", h=8), func=AF.Copy), r=[PS[7]], w=[k_tm])
            rope_tm(C, lat, lat[:, 384:416].unsqueeze(1), tb, 1, 32, kper, kper[:].unsqueeze(1), t1, t2)
            P.dve(lambda e: e.tensor_copy(out=k_tm[:, :, 64:96], in_=kper[:].unsqueeze(1).to_broadcast([128, 8, 32])), r=[kper], w=[k_tm])
            vv = v0[ti % 2]
            P.act(lambda e, vv=vv: e.activation(out=vv[:, :, 0:64], in_=PS[2][:].rearrange("p (h d) -> p h d", h=8), func=AF.Copy), r=[PS[2]], w=[vv])
            P.dma('sp', S['V0'][rows, :, :], vv[:], vv, False)
            for (src, dstl, bank, dname) in ((q_tm, qT, 3, 'QT0'), (k_tm, kT, 1, 'KT0')):
                pbh = PS[bank][:].bitcast(BF16)
                for h in range(8):
                    P.pe(lambda e, h=h, src=src, pbh=pbh: e.transpose(out=pbh[0:96, h * 128:(h + 1) * 128], in_=src[:, h, :], identity=C.idb[:]),
                         r=[src, C.idb], w=[PS[bank]])
                dt_ = dstl[ti % 2]
                P.dve(lambda e, dt_=dt_, pbh=pbh: e.tensor_copy(out=dt_[:], in_=pbh[0:96, :].rearrange("p (h t) -> p h t", h=8)), r=[PS[bank]], w=[dt_])
                P.dma('sp', S[dname][:, :, rows].rearrange("h d r -> d h r"), dt_[:], dt_, False)

        def stageC(mi):
            row0, W, isctx = mts[mi]
            X = xmT[mi % 2]; XVs = xv[mi % 2][:W // 128]
            cols = slice(row0, row0 + W)
            for ch in range(8):
                bank = 5 + (ch % 2)
                for k in range(8):
                    P.pe(lambda e, k=k, ch=ch, bank=bank, X=X, W=W: e.matmul(PS[bank][:, 0:W], lhsT=w_in[:, k, 416 + ch * 128:416 + (ch + 1) * 128], rhs=X[:, k, 0:W],
                                                                     start=(k == 0), stop=(k == 7)), r=XVs + [w_in], w=[PS[bank]])
                f_ = fm[ch % 2]
                P.dve(lambda e, f_=f_, bank=bank, W=W: e.tensor_copy(out=f_[:, 0:W], in_=PS[bank][:, 0:W]), r=[PS[bank]], w=[f_])
                P.dma('sp', S['QKPRE'][ch * 128:(ch + 1) * 128, cols], f_[:, 0:W], f_, False)
            for k in range(8):
                P.pe(lambda e, k=k, X=X, W=W: e.matmul(PS[7][0:16, 0:W], lhsT=w_in[:, k, 2464:2480], rhs=X[:, k, 0:W], start=(k == 0), stop=(k == 7)),
                     r=XVs + [w_in], w=[PS[7]])
            g_ = gsb[mi % 2]
            P.dve(lambda e, g_=g_, W=W: e.tensor_copy(out=g_[:, 0:W], in_=PS[7][0:16, 0:W]), r=[PS[7]], w=[g_])
            P.dma('sp', S['GATES'][:, cols], g_[:, 0:W], g_, False)

        stageA(0)
        for ti in range(len(tiles)):
            if ti + 1 < len(tiles):
                stageA(ti + 1)
            stageB(ti)
            mi, r = tiles[ti]
            if r == mts[mi][1] // 128 - 1:
                stageC(mi)


def phase_attn(C, pfx, QT, KT, V, nh, nkv, d, dv, scale, MIXT, with_ctx_q, merged=False, split=False):
    P, PS = C.P, C.PS
    grp = nh // nkv
    with ExitStack() as ph:
        kt = [P.sb(ph, f'{pfx}kt{i}', [128, R], BF16) for i in range(2)]
        dvl = 128 if merged else dv
        vt = [P.sb(ph, f'{pfx}vt{i}', [128, R // 128, dvl], BF16) for i in range(2)]
        sel = P.sb(ph, f'{pfx}sel', [128, 128], F32)
        rr = P.sb(ph, f'{pfx}rr', [128, 512], F32)
        if merged:
            P.pool(lambda e: e.memset(sel[:], 1.0), w=[sel])
            P.pool(lambda e: e.affine_select(out=sel[:], in_=sel[:], pattern=[[-1, 128]], base=-dv, channel_multiplier=1, compare_op=ALU.is_equal, fill=0.0),
                   r=[sel], w=[sel])
        qt = [P.sb(ph, f'{pfx}qt{i}', [128, 512], BF16) for i in range(2)]
        pt = [P.sb(ph, f'{pfx}pt{i}', [128, 512], BF16) for i in range(6)]
        ones = P.sb(ph, f'{pfx}ones', [128, 128], BF16)
        rl = P.sb(ph, f'{pfx}rl', [128, 512], F32)
        ot = [P.sb(ph, f'{pfx}ot{i}', [128, 512], BF16) for i in range(2)]
        P.pool(lambda e: e.memset(ones[:], 1.0), w=[ones])
        dve_l = (not merged) and DVE_L
        NSB = 4 if not dve_l else 3
        if dve_l:
            ones_f = P.sb(ph, f'{pfx}onesf', [128, 128], F32)
            accs = P.sb(ph, f'{pfx}accs', [128, 512], F32)
            P.pool(lambda e: e.memset(ones_f[:], 1.0), w=[ones_f])
            ab = PS[3]
        qblocks = []
        if with_ctx_q:
            qblocks.append((0, CT, CT // 128))
        if split:
            qa_t = [P.sb(ph, f'{pfx}qa{i}', [128, 512], BF16) for i in range(2)]
            qb_t = [P.sb(ph, f'{pfx}qb{i}', [128, 512], BF16) for i in range(2)]
            qtmp = P.sb(ph, f'{pfx}qtmp', [128, 512], F32)
            for m in range(TH // 512):
                qblocks.append((m * 512, 512, R // 128))
        else:
            for m in range(T // 512):
                qblocks.append((CT + m * 512, 512, R // 128))
        if C.limit_qb is not None:
            qblocks = qblocks[:C.limit_qb]
        qi = 0; pi = 0; si = 0
        pending = []
        for g in range(nkv):
            if C.limit_heads is not None and g * grp >= C.limit_heads:
                break
            kb = kt[g % 2]; vb = vt[g % 2]
            P.dma('sp', kb[0:d, :], KT[g, :, :], kb, True)
            P.dma('sp', vb[:], V[:, g, 0:dvl].rearrange("(j p) c -> p j c", p=128), vb, True)
            for hh in range(grp):
                h = g * grp + hh
                if C.limit_heads is not None and h >= C.limit_heads:
                    break
                for (q0, W, nkt) in qblocks:
                    qb = qt[qi % 2]
                    if split:
                        qa_ = qa_t[qi % 2]; qb_ = qb_t[qi % 2]
                        P.dma('sp', qa_[0:d, 0:W], QT[h, :, CT + q0:CT + q0 + W], qa_, True)
                        P.dma('sp', qb_[0:d, 0:W], QT[h, :, CT + TH + q0:CT + TH + q0 + W], qb_, True)
                        P.dve(lambda e, qa_=qa_, W=W: e.tensor_scalar(out=qtmp[0:d, 0:W], in0=qa_[0:d, 0:W], scalar1=C.sel[0:d, 0:1], scalar2=None, op0=ALU.mult),
                              r=[qa_, C.sel], w=[qtmp])
                        P.dve(lambda e, qb_=qb_, qb=qb, W=W: e.scalar_tensor_tensor(out=qb[0:d, 0:W], in0=qb_[0:d, 0:W], scalar=C.sel[0:d, 1:2], in1=qtmp[0:d, 0:W], op0=ALU.mult, op1=ALU.add),
                              r=[qb_, C.sel, qtmp], w=[qb])
                    else:
                        P.dma('sp', qb[0:d, 0:W], QT[h, :, q0:q0 + W], qb, True)
                    ob = PS[4 + 2 * (qi % 2)]; lb = PS[5 + 2 * (qi % 2)]

                    def rec_s(j, qb=qb, W=W, kb=kb):
                        nonlocal si
                        bank = PS[si % NSB]; si += 1
                        P.pe(lambda e, bank=bank, j=j: e.matmul(bank[:, 0:W], lhsT=kb[0:d, j * 128:(j + 1) * 128], rhs=qb[0:d, 0:W], start=True, stop=True),
                             r=[kb, qb], w=[bank])
                        return bank

                    def rec_pv(j, bank, W=W, vb=vb, ob=ob, lb=lb, nkt=nkt):
                        nonlocal pi
                        pb = pt[pi % 6]; pi += 1
                        P.act(lambda e, pb=pb, bank=bank: e.activation(out=pb[:, 0:W], in_=bank[:, 0:W], func=AF.Exp, scale=scale), r=[bank], w=[pb])
                        P.pe(lambda e, pb=pb, j=j: e.matmul(ob[0:dvl, 0:W], lhsT=vb[:, j, :], rhs=pb[:, 0:W], start=(j == 0), stop=(j == nkt - 1)),
                             r=[vb, pb], w=[ob])
                        if dve_l:
                            if j % 3 == 0:
                                P.pe(lambda e, pb=pb, j=j: e.matmul(lb[0:dv, 0:W], lhsT=ones[:, 0:dv], rhs=pb[:, 0:W], start=(j == 0), stop=False),
                                     r=[ones, pb], w=[lb])
                            elif j == 1:
                                P.dve(lambda e, pb=pb: e.tensor_copy(out=ab[:, 0:W], in_=pb[:, 0:W]), r=[pb], w=[ab])
                            else:
                                P.dve(lambda e, pb=pb: e.tensor_tensor(out=ab[:, 0:W], in0=ab[:, 0:W], in1=pb[:, 0:W], op=ALU.add), r=[ab, pb], w=[ab])
                        elif not merged:
                            P.pe(lambda e, pb=pb, j=j: e.matmul(lb[0:dv, 0:W], lhsT=ones[:, 0:dv], rhs=pb[:, 0:W], start=(j == 0), stop=(j == nkt - 1)),
                                 r=[ones, pb], w=[lb])

                    banks = {}
                    LA = NSB - 1
                    for j in range(min(LA, nkt)):
                        banks[j] = rec_s(j)
                    for j in range(nkt):
                        if j + LA < nkt:
                            banks[j + LA] = rec_s(j + LA)
                        rec_pv(j, banks.pop(j))
                        if j == min(EPI_DEFER, nkt - 1) and pending:
                            pending.pop(0)()
                    o_ = ot[qi % 2]
                    if dve_l:
                        P.act(lambda e, W=W: e.activation(out=accs[:, 0:W], in_=ab[:, 0:W], func=AF.Copy), r=[ab], w=[accs])

                    def epi(ob=ob, lb=lb, o_=o_, W=W, h=h, q0=q0):
                        if merged:
                            P.act(lambda e: e.activation(out=rr[0:dv, 0:W], in_=ob[0:dv, 0:W], func=AF.Copy), r=[ob], w=[rr])
                            P.dve(lambda e: e.reciprocal(out=rr[dv:128, 0:W], in_=ob[dv:128, 0:W]), r=[ob], w=[rr])
                            P.pe(lambda e: e.matmul(lb[:, 0:W], lhsT=sel[:], rhs=rr[:, 0:W], start=True, stop=True), r=[sel, rr], w=[lb])
                            P.dve(lambda e: e.tensor_tensor(out=o_[0:dv, 0:W], in0=rr[0:dv, 0:W], in1=lb[0:dv, 0:W], op=ALU.mult), r=[rr, lb], w=[o_])
                        else:
                            if dve_l:
                                P.pe(lambda e: e.matmul(lb[0:dv, 0:W], lhsT=ones_f[:, 0:dv], rhs=accs[:, 0:W], start=False, stop=True), r=[ones_f, accs], w=[lb])
                            P.dve(lambda e: e.reciprocal(out=rl[0:dv, 0:W], in_=lb[0:dv, 0:W]), r=[lb], w=[rl])
                            P.dve(lambda e: e.tensor_tensor(out=o_[0:dv, 0:W], in0=ob[0:dv, 0:W], in1=rl[0:dv, 0:W], op=ALU.mult), r=[ob, rl], w=[o_])
                        P.dma('sp', MIXT[h * dv:(h + 1) * dv, q0:q0 + W], o_[0:dv, 0:W], o_, False)
                    pending.append(epi)
                    qi += 1

        while pending:
            pending.pop(0)()

def phase_f1(C, pfx, l, MIXT, SRC, SN, UT, with_ctx, split=False):
    P, I, S, PS = C.P, C.I, C.S, C.PS
    with ExitStack() as ph:
        stg = [P.sb(ph, f'{pfx}stg{i}', [128, 1024], F32) for i in range(4)]
        C.stg_i = 0
        w_out = load_w(C, ph, f'{pfx}wo', I[f'w_out{l}'], 8, 1024, stg)
        w1 = load_w(C, ph, f'{pfx}w1', I[f'w1_{l}'], 8, 4096, stg)
        nj = 2 if with_ctx else 1
        g1 = [load_bcast(C, ph, f'{pfx}g1{j}', S['MODS'][l, j, 2 * D:3 * D], D) for j in range(nj)]
        sh = [load_bcast(C, ph, f'{pfx}sh{j}', S['MODS'][l, j, 3 * D:4 * D], D) for j in range(nj)]
        gm = [load_bcast(C, ph, f'{pfx}gm{j}', S['MODS'][l, j, 4 * D:5 * D], D) for j in range(nj)]
        mx = [P.sb(ph, f'{pfx}mx{i}', [128, 8, 512], BF16) for i in range(2)]
        sin = [P.sb(ph, f'{pfx}sin{i}', [128, D], F32) for i in range(2)]
        sn = [P.sb(ph, f'{pfx}sn{i}', [128, D], F32) for i in range(2)]
        hm = [P.sb(ph, f'{pfx}hm{i}', [128, D], BF16) for i in range(2)]
        tmp = P.sb(ph, f'{pfx}tmp', [128, D], F32)
        junk = P.sb(ph, f'{pfx}junk', [128, D], BF16)
        ss = P.sb(ph, f'{pfx}ss', [128, 1], F32); rs = P.sb(ph, f'{pfx}rs', [128, 1], F32)
        hT = [P.sb(ph, f'{pfx}hT{i}', [128, 8, 512], BF16) for i in range(2)]
        rl_ = [P.sb(ph, f'{pfx}rl{i}', [128, 512], F32) for i in range(2)]
        ut = [P.sb(ph, f'{pfx}ut{i}', [128, 512], BF16) for i in range(3)]
        ti = 0; ui = 0
        mts = mtiles() if with_ctx else mtiles()[1:]
        if split:
            mts = [(m * 512, 512, False) for m in range(TH // 512)]
            sin2 = [P.sb(ph, f'{pfx}sinb{i}', [128, D], F32) for i in range(2)]
        if C.limit_mt is not None:
            mts = mts[:C.limit_mt]
        state = {'ti': 0, 'ui': 0}

        def rowtile(mi, r):
            row0, W, isctx = mts[mi]
            j = 1 if isctx else 0
            M_ = mx[mi % 2]; H = hT[mi % 2]
            ti = state['ti']; state['ti'] += 1
            if r == 0:
                P.dma('sp', M_[:, :, 0:W], MIXT[:, row0:row0 + W].rearrange("(k p) r -> p k r", p=128), M_, True)
            rows = slice(row0 + r * 128, row0 + (r + 1) * 128)
            si_ = sin[ti % 2]; sn_ = sn[ti % 2]; hm_ = hm[ti % 2]
            if split:
                sb2 = sin2[ti % 2]
                ra = slice(CT + rows.start, CT + rows.stop); rb = slice(CT + TH + rows.start, CT + TH + rows.stop)
                P.dma('sp', si_[:], SRC[ra, :], si_, True)
                P.dma('sp', sb2[:], SRC[rb, :], sb2, True)
                P.dve(lambda e: e.tensor_scalar(out=si_[:], in0=si_[:], scalar1=C.sel[:, 0:1], scalar2=None, op0=ALU.mult), r=[si_, C.sel], w=[si_])
                P.dve(lambda e: e.scalar_tensor_tensor(out=si_[:], in0=sb2[:], scalar=C.sel[:, 1:2], in1=si_[:], op0=ALU.mult, op1=ALU.add),
                      r=[sb2, C.sel, si_], w=[si_])
            else:
                P.dma('sp', si_[:], SRC[rows, :], si_, True)
            for half in range(2):
                bank = PS[1 + half]
                for k in range(8):
                    P.pe(lambda e, k=k, half=half, bank=bank: e.matmul(bank[:], lhsT=M_[:, k, r * 128:(r + 1) * 128], rhs=w_out[:, k, half * 512:(half + 1) * 512],
                                                                     start=(k == 0), stop=(k == 7)), r=[M_, w_out], w=[bank])
                cs = slice(half * 512, (half + 1) * 512)
                P.dve(lambda e, bank=bank, cs=cs: e.tensor_tensor(out=tmp[:, cs], in0=bank[:], in1=g1[j][:, cs], op=ALU.mult), r=[bank, g1[j]], w=[tmp])
            P.dve(lambda e: e.tensor_tensor(out=sn_[:], in0=tmp[:], in1=si_[:], op=ALU.add), r=[tmp, si_], w=[sn_])
            P.dma('sp', SN[rows, :], sn_[:], sn_, False)
            P.act(lambda e: e.activation(out=junk[:], in_=sn_[:], func=AF.Square, accum_out=ss[:]), r=[sn_], w=[junk, ss])
            P.act(lambda e: e.activation(out=ss[:], in_=ss[:], func=AF.Sqrt, scale=1.0 / D, bias=EPS), r=[ss], w=[ss])
            P.dve(lambda e: e.reciprocal(out=rs[:], in_=ss[:]), r=[ss], w=[rs])
            P.dve(lambda e: e.scalar_tensor_tensor(out=tmp[:], in0=sn_[:], scalar=rs[:, 0:1], in1=gm[j][:], op0=ALU.mult, op1=ALU.mult),
                  r=[sn_, rs, gm[j]], w=[tmp])
            P.dve(lambda e: e.tensor_tensor(out=hm_[:], in0=tmp[:], in1=sh[j][:], op=ALU.add), r=[tmp, sh[j]], w=[hm_])
            pb = PS[0][:].bitcast(BF16)
            for k in range(8):
                P.pe(lambda e, k=k: e.transpose(out=pb[:, k * 128:(k + 1) * 128], in_=hm_[:, k * 128:(k + 1) * 128], identity=C.idb[:]),
                     r=[hm_, C.idb], w=[PS[0]])
            P.act(lambda e: e.activation(out=H[:, :, r * 128:(r + 1) * 128], in_=pb.rearrange("p (k t) -> p k t", k=8), func=AF.Copy),
                  r=[PS[0]], w=[H])

        def ffn_up(mi, f):
            row0, W, isctx = mts[mi]
            H = hT[mi % 2]
            bank = PS[3 + (f % 4)]
            for k in range(8):
                P.pe(lambda e, k=k: e.matmul(bank[:, 0:W], lhsT=w1[:, k, f * 128:(f + 1) * 128], rhs=H[:, k, 0:W], start=(k == 0), stop=(k == 7)),
                     r=[H, w1], w=[bank])
            ui = state['ui']; state['ui'] += 1
            r_ = rl_[f % 2]; u_ = ut[ui % 3]
            P.act(lambda e: e.activation(out=r_[:, 0:W], in_=bank[:, 0:W], func=AF.Relu), r=[bank], w=[r_])
            P.dve(lambda e: e.tensor_tensor(out=u_[:, 0:W], in0=r_[:, 0:W], in1=r_[:, 0:W], op=ALU.mult), r=[r_], w=[u_])
            P.dma('sp', UT[f * 128:(f + 1) * 128, row0:row0 + W], u_[:, 0:W], u_, False)

        for r in range(mts[0][1] // 128):
            rowtile(0, r)
        for mi in range(len(mts)):
            nxt = (mts[mi + 1][1] // 128) if mi + 1 < len(mts) else 0
            for q in range(4):
                for f in range(8 * q, 8 * q + 8):
                    ffn_up(mi, f)
                if q < nxt:
                    rowtile(mi + 1, q)


def phase_f2(C, pfx, l, SN, UT, DST, with_ctx, final, split=False):
    P, I, S, PS = C.P, C.I, C.S, C.PS
    with ExitStack() as ph:
        stg = [P.sb(ph, f'{pfx}stg{i}', [128, 1024], F32) for i in range(4)]
        C.stg_i = 0
        w2 = load_w(C, ph, f'{pfx}w2', I[f'w2_{l}'], 32, 1024, stg)
        nj = 2 if with_ctx else 1
        g2 = [load_bcast(C, ph, f'{pfx}g2{j}', S['MODS'][l, j, 5 * D:6 * D], D) for j in range(nj)]
        fn = load_bcast(C, ph, f'{pfx}fn', I['final_norm'], D) if final else None
        ub = [P.sb(ph, f'{pfx}ub{i}', [128, 32, 256], BF16) for i in range(2)]
        sn = [P.sb(ph, f'{pfx}sn{i}', [128, D], F32) for i in range(2)]
        so = [P.sb(ph, f'{pfx}so{i}', [128, D], F32) for i in range(2)]
        tmp = P.sb(ph, f'{pfx}tmp', [128, D], F32)
        junk = P.sb(ph, f'{pfx}junk', [128, D], BF16)
        ss = P.sb(ph, f'{pfx}ss', [128, 1], F32); rs = P.sb(ph, f'{pfx}rs', [128, 1], F32)
        ti = 0
        mts = [(r0, 256, r0 < CT) for r0 in range(0 if with_ctx else CT, R, 256)]
        yoff = -CT
        if split:
            mts = [(r0, 256, False) for r0 in range(0, TH, 256)]; yoff = 0
        if C.limit_mt is not None:
            mts = mts[:2 * C.limit_mt - (1 if with_ctx else 0)]
        for mi, (row0, W, isctx) in enumerate(mts):
            j = 1 if isctx else 0
            U = ub[mi % 2]
            for fq in range(4):
                P.dma('sp', U[:, fq * 8:(fq + 1) * 8, 0:W], UT[fq * 1024:(fq + 1) * 1024, row0:row0 + W].rearrange("(f p) r -> p f r", p=128), U, True)
            for r in range(W // 128):
                rows = slice(row0 + r * 128, row0 + (r + 1) * 128)
                sn_ = sn[ti % 2]; so_ = so[ti % 2]
                P.dma('sp', sn_[:], SN[rows, :], sn_, True)
                for half in range(2):
                    bank = PS[2 * (ti % 2) + half]
                    for f in range(32):
                        P.pe(lambda e, f=f, half=half, bank=bank, U=U, r=r: e.matmul(bank[:], lhsT=U[:, f, r * 128:(r + 1) * 128], rhs=w2[:, f, half * 512:(half + 1) * 512],
                                                                              start=(f == 0), stop=(f == 31)), r=[U, w2], w=[bank])
                    cs = slice(half * 512, (half + 1) * 512)
                    P.dve(lambda e, bank=bank, cs=cs, j=j: e.tensor_tensor(out=tmp[:, cs], in0=bank[:], in1=g2[j][:, cs], op=ALU.mult), r=[bank, g2[j]], w=[tmp])
                P.dve(lambda e, sn_=sn_, so_=so_: e.tensor_tensor(out=so_[:], in0=tmp[:], in1=sn_[:], op=ALU.add), r=[tmp, sn_], w=[so_])
                if final:
                    P.act(lambda e, so_=so_: e.activation(out=junk[:], in_=so_[:], func=AF.Square, accum_out=ss[:]), r=[so_], w=[junk, ss])
                    P.act(lambda e: e.activation(out=ss[:], in_=ss[:], func=AF.Sqrt, scale=1.0 / D, bias=EPS), r=[ss], w=[ss])
                    P.dve(lambda e: e.reciprocal(out=rs[:], in_=ss[:]), r=[ss], w=[rs])
                    P.dve(lambda e, so_=so_: e.scalar_tensor_tensor(out=so_[:], in0=so_[:], scalar=rs[:, 0:1], in1=fn[:], op0=ALU.mult, op1=ALU.mult),
                          r=[so_, rs, fn], w=[so_])
                    P.dma('sp', C.y[row0 + yoff + r * 128:row0 + yoff + (r + 1) * 128, :], so_[:], so_, False)
                else:
                    P.dma('sp', DST[rows, :], so_[:], so_, False)
                ti += 1


def phase_a1(C):
    P, I, S, PS = C.P, C.I, C.S, C.PS
    S['QT1'] = C.dscr('QT1', [8, 128, R], BF16)
    S['KT1'] = C.dscr('KT1', [2, 128, R], BF16)
    S['V1'] = C.dscr('V1', [R, 2, 128], BF16)
    with ExitStack() as ph:
        stg = [P.sb(ph, f'b_stg{i}', [128, 1536], F32) for i in range(4)]
        C.stg_i = 0
        w_in = load_w(C, ph, 'b_win', I['w_in1'], 8, 1536, stg)
        gm = [load_bcast(C, ph, f'b_gm{j}', S['MODS'][1, j, 1 * D:2 * D], D) for j in range(2)]
        sh = [load_bcast(C, ph, f'b_sh{j}', S['MODS'][1, j, 0 * D:1 * D], D) for j in range(2)]
        qn = load_bcast(C, ph, 'b_qn', I['q_norm1'], 128)
        kn = load_bcast(C, ph, 'b_kn', I['k_norm1'], 128)
        B = norm_bufs(C, ph, 'b_')
        Xs = [P.sb(ph, f'b_xmT{i}', [128, 8, 128], BF16) for i in range(2)]
        tab = [P.sb(ph, f'b_tab{i}', [128, 2, 128], F32) for i in range(3)]
        qfs = [P.sb(ph, f'b_qf{i}', [128, 10, 128], F32) for i in range(2)]
        sq = P.sb(ph, 'b_sq', [128, 10, 128], F32)
        ssq = P.sb(ph, 'b_ssq', [128, 10], F32); rr = P.sb(ph, 'b_rr', [128, 10], F32)
        t1 = P.sb(ph, 'b_t1', [128, 1280], F32); t2 = P.sb(ph, 'b_t2', [128, 1280], F32)
        qk_tm = P.sb(ph, 'b_qktm', [128, 10, 128], BF16)
        v1 = [P.sb(ph, f'b_v1{i}', [128, 256], BF16) for i in range(2)]
        qkT = [P.sb(ph, f'b_qkT{i}', [128, 10, 128], BF16) for i in range(2)]
        nrt = R // 128
        if C.limit_mt is not None:
            nrt = 2 + 4 * (C.limit_mt - 1)
        def stageA(ti):
            rows = slice(ti * 128, (ti + 1) * 128)
            j = 1 if ti < 2 else 0
            norm_mod_rows(C, B, ti, S['S1'][rows, :], gm[j], sh[j], Xs[ti % 2], 0)
            tb = tab[ti % 3]
            P.dma('sp', tb[:], I['ropeG'][rows, :, :], tb, True)

        def stageB1(ti):
            rows = slice(ti * 128, (ti + 1) * 128)
            isctx = ti < 2
            X = Xs[ti % 2]; qf = qfs[ti % 2]
            h0 = 8 if isctx else 0
            if not isctx:
                for half in range(2):
                    for k in range(8):
                        P.pe(lambda e, k=k, half=half: e.matmul(PS[1 + half][:], lhsT=X[:, k, :], rhs=w_in[:, k, half * 512:(half + 1) * 512], start=(k == 0), stop=(k == 7)),
                             r=[X, w_in], w=[PS[1 + half]])
                    P.act(lambda e, half=half: e.activation(out=qf[:, half * 4:(half + 1) * 4, :], in_=PS[1 + half][:].rearrange("p (h d) -> p h d", h=4), func=AF.Copy),
                          r=[PS[1 + half]], w=[qf])
            for k in range(8):
                P.pe(lambda e, k=k: e.matmul(PS[3][:], lhsT=X[:, k, :], rhs=w_in[:, k, 1024:1536], start=(k == 0), stop=(k == 7)), r=[X, w_in], w=[PS[3]])
            P.act(lambda e: e.activation(out=qf[:, 8:10, :], in_=PS[3][:, 0:256].rearrange("p (h d) -> p h d", h=2), func=AF.Copy), r=[PS[3]], w=[qf])
            vv = v1[ti % 2]
            P.act(lambda e, vv=vv: e.activation(out=vv[:], in_=PS[3][:, 256:512], func=AF.Copy), r=[PS[3]], w=[vv])
            P.dma('sp', S['V1'][rows, :, :].rearrange("r h d -> r (h d)"), vv[:], vv, False)

        def stageB2(ti):
            rows = slice(ti * 128, (ti + 1) * 128)
            isctx = ti < 2
            tb = tab[ti % 3]; qf = qfs[ti % 2]
            h0 = 8 if isctx else 0
            nh = 10 - h0
            P.dve(lambda e, h0=h0: e.tensor_tensor(out=sq[:, h0:10, :], in0=qf[:, h0:10, :], in1=qf[:, h0:10, :], op=ALU.mult), r=[qf], w=[sq])
            P.dve(lambda e, h0=h0: e.tensor_reduce(out=ssq[:, h0:10], in_=sq[:, h0:10, :], axis=AX.X, op=ALU.add), r=[sq], w=[ssq])
            P.act(lambda e, h0=h0: e.activation(out=ssq[:, h0:10], in_=ssq[:, h0:10], func=AF.Sqrt, scale=1.0 / 128, bias=EPS), r=[ssq], w=[ssq])
            P.dve(lambda e, h0=h0: e.reciprocal(out=rr[:, h0:10], in_=ssq[:, h0:10]), r=[ssq], w=[rr])
            P.dve(lambda e, h0=h0, nh=nh: e.tensor_tensor(out=qf[:, h0:10, :], in0=qf[:, h0:10, :], in1=rr[:, h0:10].unsqueeze(2).to_broadcast([128, nh, 128]), op=ALU.mult),
                  r=[qf, rr], w=[qf])
            if not isctx:
                P.dve(lambda e: e.tensor_tensor(out=qf[:, 0:8, :], in0=qf[:, 0:8, :], in1=qn[:].unsqueeze(1).to_broadcast([128, 8, 128]), op=ALU.mult), r=[qf, qn], w=[qf])
            P.dve(lambda e: e.tensor_tensor(out=qf[:, 8:10, :], in0=qf[:, 8:10, :], in1=kn[:].unsqueeze(1).to_broadcast([128, 2, 128]), op=ALU.mult), r=[qf, kn], w=[qf])
            rope_tm(C, qf, qf[:, h0:10, :], tb, nh, 128, qk_tm, qk_tm[:, h0:10, :], t1, t2)
            oT = qkT[ti % 2]
            for (a, b_, bank) in ((0, 4, 4), (4, 8, 5), (8, 10, 6)):
                if a < h0:
                    continue
                pbh = PS[bank][:].bitcast(BF16)
                for h in range(a, b_):
                    P.pe(lambda e, h=h, a=a, pbh=pbh: e.transpose(out=pbh[:, (h - a) * 128:(h - a + 1) * 128], in_=qk_tm[:, h, :], identity=C.idb[:]),
                         r=[qk_tm, C.idb], w=[PS[bank]])
                P.dve(lambda e, a=a, b_=b_, pbh=pbh, oT=oT: e.tensor_copy(out=oT[:, a:b_, :], in_=pbh[:, 0:(b_ - a) * 128].rearrange("p (h t) -> p h t", h=b_ - a)),
                      r=[PS[bank]], w=[oT])
            if not isctx:
                P.dma('sp', S['QT1'][:, :, rows].rearrange("h d r -> d h r"), oT[:, 0:8, :], oT, False)
            P.dma('sp', S['KT1'][:, :, rows].rearrange("h d r -> d h r"), oT[:, 8:10, :], oT, False)

        stageA(0)
        for ti in range(nrt):
            if ti + 1 < nrt:
                stageA(ti + 1)
            stageB1(ti)
            if ti >= 1:
                stageB2(ti - 1)
        stageB2(nrt - 1)


def phase_ml_prep(C):
    P, I, S, PS = C.P, C.I, C.S, C.PS
    S['QMLT'] = C.dscr('QMLT', [4, 128, R], BF16)
    S['KMLT'] = C.dscr('KMLT', [4, 128, R], BF16)
    S['KTM'] = C.dscr('KTM', [R, 4, 128], BF16)
    with ExitStack() as ph:
        cw = P.sb(ph, 'c_cw', [128, 8, 3], F32)
        with C.nc.allow_non_contiguous_dma(reason="tiny conv weight transpose"):
            pass
        xin = [P.sb(ph, f'c_xin{i}', [128, 514], F32) for i in range(2)]
        accs_ = [P.sb(ph, f'c_acc{i}', [128, 512], F32) for i in range(2)]
        sls_ = [P.sb(ph, f'c_sl{i}', [128, 512], F32) for i in range(2)]
        ob = [P.sb(ph, f'c_ob{i}', [128, 512], BF16) for i in range(2)]
        ktm = [P.sb(ph, f'c_ktm{i}', [128, 4, 128], BF16) for i in range(2)]
        for j in range(3):
            for cc in range(8):
                pass
        P.dma('sp', cw[:], I['convT'][:, :, :], cw, True)
        it = 0
        segs = [(0, CT, 0, CT)] + [(CT + m * 512, 512, CT, R) for m in range(T // 512)]
        if C.limit_mt is not None:
            segs = segs[:C.limit_mt]
        for (c0, W, lo, hi) in segs:
            for cc in range(8):
                x_ = xin[it % 2]; o_ = ob[it % 2]; acc = accs_[it % 2]; sl = sls_[it % 2]; it += 1
                a = max(c0 - 1, lo); b = min(c0 + W + 1, hi)
                if a > c0 - 1:
                    P.pool(lambda e, x_=x_: e.memset(x_[:, 0:1], 0.0), w=[x_])
                if b < c0 + W + 1:
                    P.pool(lambda e, x_=x_, W=W: e.memset(x_[:, W + 1:W + 2], 0.0), w=[x_])
                P.dma('sp', x_[:, a - (c0 - 1):b - (c0 - 1)], S['QKPRE'][cc * 128:(cc + 1) * 128, a:b], x_, True)
                P.dve(lambda e, x_=x_, W=W, cc=cc, acc=acc: e.tensor_scalar(out=acc[:, 0:W], in0=x_[:, 0:W], scalar1=cw[:, cc, 0:1], scalar2=None, op0=ALU.mult), r=[x_, cw], w=[acc])
                P.dve(lambda e, x_=x_, W=W, cc=cc, acc=acc: e.scalar_tensor_tensor(out=acc[:, 0:W], in0=x_[:, 1:W + 1], scalar=cw[:, cc, 1:2], in1=acc[:, 0:W], op0=ALU.mult, op1=ALU.add),
                      r=[x_, cw, acc], w=[acc])
                P.dve(lambda e, x_=x_, W=W, cc=cc, acc=acc: e.scalar_tensor_tensor(out=acc[:, 0:W], in0=x_[:, 2:W + 2], scalar=cw[:, cc, 2:3], in1=acc[:, 0:W], op0=ALU.mult, op1=ALU.add),
                      r=[x_, cw, acc], w=[acc])
                if cc < 4:
                    P.act(lambda e, W=W, acc=acc, sl=sl: e.activation(out=sl[:, 0:W], in_=acc[:, 0:W], func=AF.Silu), r=[acc], w=[sl])
                    P.act(lambda e, o_=o_, W=W, sl=sl: e.activation(out=o_[:, 0:W], in_=sl[:, 0:W], func=AF.Copy, scale=float(128 ** -0.5)), r=[sl], w=[o_])
                    P.dma('sp', S['QMLT'][cc, :, c0:c0 + W], o_[:, 0:W], o_, False)
                else:
                    P.act(lambda e, o_=o_, W=W, acc=acc: e.activation(out=o_[:, 0:W], in_=acc[:, 0:W], func=AF.Silu), r=[acc], w=[o_])
                    P.dma('sp', S['KMLT'][cc - 4, :, c0:c0 + W], o_[:, 0:W], o_, False)
                    pb = PS[(cc % 2)][:].bitcast(BF16)
                    for r in range(W // 128):
                        P.pe(lambda e, r=r, o_=o_, pb=pb: e.transpose(out=pb[:, r * 128:(r + 1) * 128], in_=o_[:, r * 128:(r + 1) * 128], identity=C.idb[:]),
                             r=[o_, C.idb], w=[PS[cc % 2]])
                    kt_ = ktm[cc % 2]
                    nr = W // 128
                    P.dve(lambda e, kt_=kt_, pb=pb, nr=nr: e.tensor_copy(out=kt_[:, 0:nr, :], in_=pb[:, 0:nr * 128].rearrange("p (r d) -> p r d", r=nr)), r=[PS[cc % 2]], w=[kt_])
                    P.dma('sp', S['KTM'][c0:c0 + W, cc - 4, :].rearrange("(r p) d -> p r d", p=128), kt_[:, 0:nr, :], kt_, False)


def phase_ml_scan(C):
    P, I, S, PS = C.P, C.I, C.S, C.PS
    NCH = R // 128
    S['HAB'] = C.dscr('HAB', [2, R, 4, 128], F32)
    S['WST'] = C.dscr('WST', [4, 2 * NCH], F32)
    nch = NCH if C.limit_mt is None else 2 + 4 * (C.limit_mt - 1)
    orderA = list(range(nch))
    orderB = [1, 0] + list(range(nch - 1, 1, -1))
    with ExitStack() as ph:
        li = P.sb(ph, 's_li', [4, R], F32)
        gf = P.sb(ph, 's_gf', [4, R], F32)
        pp = P.sb(ph, 's_pp', [4, R], F32)
        gb = P.sb(ph, 's_gb', [4, 4], F32); ngb = P.sb(ph, 's_ngb', [4, 4], F32)
        one4 = P.sb(ph, 's_one', [4, 128], F32)
        tot = P.sb(ph, 's_tot', [4, 2, NCH], F32)
        amax = P.sb(ph, 's_amax', [4, 2, NCH], F32)
        Mc = P.sb(ph, 's_Mc', [4, 2, NCH], F32)
        Min = P.sb(ph, 's_Min', [4, 2, NCH], F32)
        mcur = P.sb(ph, 's_mcur', [4, 2], F32)
        wsf = P.sb(ph, 's_wsf', [4, 2, NCH], F32)
        etm = P.sb(ph, 's_etm', [128, 2, 2, NCH, 4], F32)
        wsb = P.sb(ph, 's_wsb', [128, 4, 2, NCH], F32)
        maskA = P.sb(ph, 's_mA', [128, 128], F32); maskB = P.sb(ph, 's_mB', [128, 128], F32)
        G = S['GATES'].rearrange("(d i h) r -> h d i r", d=2, i=2)
        P.dma('sp', gb[:], I['gate_bT'][:, :], gb, True)
        P.dve(lambda e: e.tensor_scalar(out=ngb[:], in0=gb[:], scalar1=-1.0, scalar2=None, op0=ALU.mult), r=[gb], w=[ngb])
        P.pool(lambda e: e.memset(one4[:], 1.0), w=[one4])
        P.pool(lambda e: e.memset(maskA[:], 1.0), w=[maskA])
        P.pool(lambda e: e.affine_select(out=maskA[:], in_=maskA[:], pattern=[[1, 128]], base=0, channel_multiplier=-1, compare_op=ALU.is_ge, fill=0.0),
               r=[maskA], w=[maskA])
        P.pool(lambda e: e.memset(maskB[:], 1.0), w=[maskB])
        P.pool(lambda e: e.affine_select(out=maskB[:], in_=maskB[:], pattern=[[-1, 128]], base=0, channel_multiplier=1, compare_op=ALU.is_ge, fill=0.0),
               r=[maskB], w=[maskB])
        P.pool(lambda e: e.memset(mcur[:], 0.0), w=[mcur])
        P.pool(lambda e: e.memset(Mc[:], 0.0), w=[Mc])
        P.pool(lambda e: e.memset(Min[:], 0.0), w=[Min])
        P.pool(lambda e: e.memset(tot[:], 0.0), w=[tot])
        P.pool(lambda e: e.memset(etm[:], 0.0), w=[etm])
        P3 = lambda t_: t_[:, 0:nch * 128].rearrange("h (c s) -> h c s", s=128)
        NR = nch * 128
        for d, order in ((0, orderA), (1, orderB)):
            P.dma('sp', li[:, 0:NR], G[:, d, 0, 0:NR], li, True)
            P.dma('sp', gf[:, 0:NR], G[:, d, 1, 0:NR], gf, True)
            P.dve(lambda e, d=d: e.tensor_scalar(out=li[:, 0:NR], in0=li[:, 0:NR], scalar1=gb[:, 2 * d:2 * d + 1], scalar2=None, op0=ALU.add), r=[li, gb], w=[li])
            P.act(lambda e, d=d: e.activation(out=gf[:, 0:NR], in_=gf[:, 0:NR], func=AF.Exp, scale=-1.0, bias=ngb[:, 2 * d + 1:2 * d + 2]), r=[gf, ngb], w=[gf])
            P.act(lambda e: e.activation(out=gf[:, 0:NR], in_=gf[:, 0:NR], func=AF.Ln, scale=1.0, bias=1.0), r=[gf], w=[gf])
            for c in range(nch):
                cs = slice(c * 128, (c + 1) * 128)
                P.dve(lambda e, cs=cs: e.tensor_tensor_scan(out=pp[:, cs], data0=one4[:], data1=gf[:, cs], initial=0.0, op0=ALU.mult, op1=ALU.add),
                      r=[one4, gf], w=[pp])
            P.dve(lambda e, d=d: e.tensor_copy(out=tot[:, d, 0:nch], in_=P3(pp)[:, :, 127]), r=[pp], w=[tot])
            if d == 1:
                P.dve(lambda e: e.tensor_tensor(out=pp[:, 0:NR], in0=gf[:, 0:NR], in1=pp[:, 0:NR], op=ALU.subtract), r=[gf, pp], w=[pp])
                P.dve(lambda e: e.tensor_tensor(out=P3(pp), in0=P3(pp), in1=tot[:, 1, 0:nch].unsqueeze(2).to_broadcast([4, nch, 128]), op=ALU.add), r=[pp, tot], w=[pp])
            P.dve(lambda e: e.tensor_tensor(out=li[:, 0:NR], in0=li[:, 0:NR], in1=pp[:, 0:NR], op=ALU.add), r=[li, pp], w=[li])
            P.dve(lambda e, d=d: e.tensor_reduce(out=amax[:, d, 0:nch], in_=P3(li), axis=AX.X, op=ALU.max), r=[li], w=[amax])
            for c in order:
                P.dve(lambda e, d=d, c=c: e.tensor_copy(out=Min[:, d, c:c + 1], in_=mcur[:, d:d + 1]), r=[mcur], w=[Min])
                P.dve(lambda e, d=d, c=c: e.tensor_tensor(out=Mc[:, d, c:c + 1], in0=mcur[:, d:d + 1], in1=amax[:, d, c:c + 1], op=ALU.max), r=[mcur, amax], w=[Mc])
                P.dve(lambda e, d=d, c=c: e.tensor_tensor(out=mcur[:, d:d + 1], in0=Mc[:, d, c:c + 1], in1=tot[:, d, c:c + 1], op=ALU.subtract), r=[Mc, tot], w=[mcur])
            for ai, t_ in enumerate((li, pp)):
                P.dve(lambda e, t_=t_, d=d: e.tensor_tensor(out=P3(t_), in0=P3(t_), in1=Mc[:, d, 0:nch].unsqueeze(2).to_broadcast([4, nch, 128]), op=ALU.subtract),
                      r=[t_, Mc], w=[t_])
                P.act(lambda e, t_=t_: e.activation(out=t_[:, 0:NR], in_=t_[:, 0:NR], func=AF.Exp), r=[t_], w=[t_])
                bank = PS[ai * 2 + d]
                for c in range(nch):
                    P.pe(lambda e, t_=t_, c=c, bank=bank: e.transpose(out=bank[:, c * 4:(c + 1) * 4], in_=t_[:, c * 128:(c + 1) * 128], identity=C.idf[0:4, 0:4]),
                         r=[t_, C.idf], w=[bank])
                P.dve(lambda e, ai=ai, d=d, bank=bank: e.tensor_copy(out=etm[:, ai, d, 0:nch, :], in_=bank[:, 0:nch * 4].rearrange("p (c h) -> p c h", h=4)),
                      r=[bank], w=[etm])
        P.dve(lambda e: e.tensor_tensor(out=wsf[:], in0=Min[:], in1=Mc[:], op=ALU.subtract), r=[Min, Mc], w=[wsf])
        P.act(lambda e: e.activation(out=wsf[:], in_=wsf[:], func=AF.Exp), r=[wsf], w=[wsf])
        P.dma('sp', S['WST'][:, :], wsf[:].rearrange("h d c -> h (d c)"), wsf, False)
        P.barrier()
        P.dma('sp', wsb[:], S['WST'].rearrange("h x -> (h x)").partition_broadcast(128), wsb, True)
        NB = 3
        qT = [P.sb(ph, f's_qT{i}', [128, 4, 128], BF16) for i in range(NB)]
        kT = [P.sb(ph, f's_kT{i}', [128, 4, 128], BF16) for i in range(NB)]
        kM = [P.sb(ph, f's_kM{i}', [128, 4, 128], BF16) for i in range(NB)]
        vM = [P.sb(ph, f's_vM{i}', [128, 4, 128], BF16) for i in range(NB)]
        hO = [P.sb(ph, f's_hO{i}', [128, 4, 128], F32) for i in range(NB)]
        Cst = [[P.sb(ph, f's_C{d}{h}', [128, 129], F32) for h in range(4)] for d in range(2)]
        Cbf = [[P.sb(ph, f's_Cb{d}{h}', [128, 129], BF16) for h in range(4)] for d in range(2)]
        Sm = [P.sb(ph, f's_Sm{i}', [128, 128], BF16) for i in range(4)]
        vE = [P.sb(ph, f's_vE{i}', [128, 129], BF16) for i in range(4)]
        dn = [P.sb(ph, f's_dn{i}', [128, 1], F32) for i in range(4)]
        for d in range(2):
            for h in range(4):
                P.pool(lambda e, d=d, h=h: e.memset(Cst[d][h][:], 0.0), w=[Cst[d][h]])
        bi = 0; ui = 0
        pend = []
        bNs = (PS[6], PS[0], PS[2]); bCs = (PS[7], PS[1], PS[3])
        for step in range(nch):
            for d, order, mask in ((0, orderA, maskA), (1, orderB, maskB)):
                c = order[step]
                rows = slice(c * 128, (c + 1) * 128)
                b_ = bi % NB; bi += 1
                P.dma('sp', qT[b_][:], S['QMLT'][:, :, rows].rearrange("h d r -> d h r"), qT[b_], True)
                P.dma('sp', kT[b_][:], S['KMLT'][:, :, rows].rearrange("h d r -> d h r"), kT[b_], True)
                P.dma('sp', kM[b_][:], S['KTM'][rows, :, :], kM[b_], True)
                P.dma('sp', vM[b_][:], S['VML'][rows, :].rearrange("r (h d) -> r h d", h=4), vM[b_], True)
                for h in range(4):
                    u = ui % 4; u3 = ui % 3; ui += 1
                    Cs = Cst[d][h]; Cb = Cbf[d][h]
                    wsc = wsb[:, h, d, c:c + 1]
                    bS = PS[4 + (u % 2)]; bN = bNs[u3]; bC = bCs[u3]
                    P.dve(lambda e, Cs=Cs, Cb=Cb, wsc=wsc: e.tensor_scalar(out=Cb[:], in0=Cs[:], scalar1=wsc, scalar2=None, op0=ALU.mult), r=[Cs, wsb], w=[Cb])
                    P.pe(lambda e, b_=b_, h=h, bS=bS: e.matmul(bS[:, 0:128], lhsT=kT[b_][:, h, :], rhs=qT[b_][:, h, :], start=True, stop=True), r=[kT[b_], qT[b_]], w=[bS])
                    P.dve(lambda e, u=u, bS=bS, mask=mask: e.tensor_tensor(out=Sm[u][:], in0=bS[:, 0:128], in1=mask[:], op=ALU.mult), r=[bS, mask], w=[Sm[u]])
                    P.dve(lambda e, u=u, b_=b_, h=h, d=d, c=c: e.tensor_scalar(out=vE[u][:, 0:128], in0=vM[b_][:, h, :], scalar1=etm[:, 0, d, c, h:h + 1], scalar2=None, op0=ALU.mult),
                          r=[vM[b_], etm], w=[vE[u]])
                    P.act(lambda e, u=u, d=d, c=c, h=h: e.activation(out=vE[u][:, 128:129], in_=etm[:, 0, d, c, h:h + 1], func=AF.Copy), r=[etm], w=[vE[u]])
                    P.pe(lambda e, u=u, bN=bN: e.matmul(bN[:, 0:129], lhsT=Sm[u][:], rhs=vE[u][:], start=True, stop=False), r=[Sm[u], vE[u]], w=[bN])
                    P.pe(lambda e, b_=b_, h=h, Cb=Cb, bN=bN: e.matmul(bN[:, 0:129], lhsT=qT[b_][:, h, :], rhs=Cb[:], start=False, stop=True), r=[qT[b_], Cb], w=[bN])
                    P.pe(lambda e, u=u, b_=b_, h=h, bC=bC: e.matmul(bC[:, 0:129], lhsT=kM[b_][:, h, :], rhs=vE[u][:], start=True, stop=True), r=[kM[b_], vE[u]], w=[bC])

                    def tail(u=u, h=h, d=d, c=c, b_=b_, Cs=Cs, wsc=wsc, bN=bN, bC=bC, rows=rows):
                        P.dve(lambda e: e.scalar_tensor_tensor(out=Cs[:], in0=Cs[:], scalar=wsc, in1=bC[:, 0:129], op0=ALU.mult, op1=ALU.add),
                              r=[Cs, wsb, bC], w=[Cs])
                        P.act(lambda e: e.activation(out=dn[u][:], in_=bN[:, 128:129], func=AF.Abs), r=[bN], w=[dn[u]])
                        P.dve(lambda e: e.tensor_tensor(out=dn[u][:], in0=dn[u][:], in1=etm[:, 1, d, c, h:h + 1], op=ALU.max),
                              r=[dn[u], etm], w=[dn[u]])
                        P.dve(lambda e: e.reciprocal(out=dn[u][:], in_=dn[u][:]), r=[dn[u]], w=[dn[u]])
                        P.act(lambda e: e.activation(out=hO[b_][:, h, :], in_=bN[:, 0:128], func=AF.Copy, scale=dn[u][:, 0:1]), r=[bN, dn[u]], w=[hO[b_]])
                        if h == 3:
                            P.dma('sp', S['HAB'][d, rows, :, :], hO[b_][:], hO[b_], False)
                    pend.append(tail)
                    if len(pend) > 2:
                        pend.pop(0)()
        while pend:
            pend.pop(0)()


def phase_ml_out(C, MIXT):
    P, I, S, PS = C.P, C.I, C.S, C.PS
    nch = R // 128 if C.limit_mt is None else 2 + 4 * (C.limit_mt - 1)
    with ExitStack() as ph:
        hn = load_bcast(C, ph, 'o_hn', I['head_norm'], 512)
        ha = [P.sb(ph, f'o_ha{i}', [128, 4, 128], F32) for i in range(2)]
        hb = [P.sb(ph, f'o_hb{i}', [128, 4, 128], F32) for i in range(2)]
        og = [P.sb(ph, f'o_og{i}', [128, 512], F32) for i in range(2)]
        sq = P.sb(ph, 'o_sq', [128, 4, 128], F32)
        ssq = P.sb(ph, 'o_ssq', [128, 4], F32); rr = P.sb(ph, 'o_rr', [128, 4], F32)
        mb = P.sb(ph, 'o_mb', [128, 4, 128], BF16)
        mT = [P.sb(ph, f'o_mT{i}', [128, 4, 128], BF16) for i in range(2)]
        for c in range(nch):
            rows = slice(c * 128, (c + 1) * 128)
            a = ha[c % 2]; b = hb[c % 2]; o = og[c % 2]
            P.dma('sp', a[:], S['HAB'][0, rows, :, :], a, True)
            P.dma('sp', b[:], S['HAB'][1, rows, :, :], b, True)
            P.dma('sp', o[:], S['OSIG'][rows, :], o, True)
            P.dve(lambda e, a=a, b=b: e.tensor_tensor(out=a[:], in0=a[:], in1=b[:], op=ALU.add), r=[a, b], w=[a])
            P.dve(lambda e, a=a: e.tensor_tensor(out=sq[:], in0=a[:], in1=a[:], op=ALU.mult), r=[a], w=[sq])
            P.dve(lambda e: e.tensor_reduce(out=ssq[:], in_=sq[:], axis=AX.X, op=ALU.add), r=[sq], w=[ssq])
            P.act(lambda e: e.activation(out=ssq[:], in_=ssq[:], func=AF.Sqrt, scale=1.0 / 128, bias=EPS), r=[ssq], w=[ssq])
            P.dve(lambda e: e.reciprocal(out=rr[:], in_=ssq[:]), r=[ssq], w=[rr])
            P.dve(lambda e, a=a: e.tensor_tensor(out=a[:], in0=a[:], in1=rr[:].unsqueeze(2).to_broadcast([128, 4, 128]), op=ALU.mult), r=[a, rr], w=[a])
            P.dve(lambda e, o=o: e.tensor_tensor(out=o[:], in0=o[:], in1=hn[:], op=ALU.mult), r=[o, hn], w=[o])
            P.dve(lambda e, a=a, o=o: e.tensor_tensor(out=mb[:], in0=a[:], in1=o[:].rearrange("p (h d) -> p h d", h=4), op=ALU.mult), r=[a, o], w=[mb])
            pb = PS[c % 2][:].bitcast(BF16)
            for h in range(4):
                P.pe(lambda e, h=h, pb=pb: e.transpose(out=pb[:, h * 128:(h + 1) * 128], in_=mb[:, h, :], identity=C.idb[:]), r=[mb, C.idb], w=[PS[c % 2]])
            m_ = mT[c % 2]
            P.act(lambda e, m_=m_, pb=pb: e.activation(out=m_[:], in_=pb[:, 0:512].rearrange("p (h t) -> p h t", h=4), func=AF.Copy), r=[PS[c % 2]], w=[m_])
            P.dma('sp', MIXT[512:1024, rows].rearrange("(h d) r -> d h r", h=4), m_[:], m_, False)


def rope_tables():
    def ang(drot):
        nf = drot // 4
        freqs = (10000.0 ** (-np.arange(nf, dtype=np.float32) / nf)).astype(np.float32)
        t = np.arange(T)
        pos = np.stack([t // 64, t % 64], -1).astype(np.float32)
        return pos[:, :, None] * freqs

    out = {}
    for name, drot in (('ropeM', 32), ('ropeG', 128)):
        a = ang(drot)
        nf = drot // 4
        cos = np.cos(a)[:, :, None, :].repeat(2, axis=2)
        sin = np.sin(a)
        ssin = np.stack([-sin, sin], axis=2)
        tab = np.zeros((R, 2, drot), np.float32)
        tab[:CT, 0, :] = 1.0
        tab[CT:, 0, :] = cos.reshape(T, drot)
        tab[CT:, 1, :] = ssin.reshape(T, drot)
        out[name] = tab
    return out


def make_in_maps(inp):
    f = lambda a: np.ascontiguousarray(np.asarray(a, dtype=np.float32))
    rt = rope_tables()
    w_uq = f(inp['l0_mla_w_uq']).reshape(256, 8, 96)
    w_uq = np.concatenate([w_uq[:, :, :64].reshape(256, 512), w_uq[:, :, 64:].reshape(256, 256)], axis=1)
    w_ukv = f(inp['l0_mla_w_ukv']).reshape(128, 8, 128)
    w_ukv = np.concatenate([w_ukv[:, :, :64].reshape(128, 512), w_ukv[:, :, 64:].reshape(128, 512)], axis=1)
    shared = {
        'w_in0': f(inp['l0_w_in']), 'w_in1': f(inp['l1_w_in']),
        'q_norm0': f(inp['l0_mla_q_norm']), 'kv_norm0': f(inp['l0_mla_kv_norm']),
        'w_uq': f(w_uq), 'w_ukv': f(w_ukv), 'convT': f(f(inp['l0_ml_conv']).reshape(3, 8, 128).transpose(2, 1, 0)),
        'gate_bT': f(f(inp['l0_ml_gate_b']).reshape(2, 2, 4).transpose(2, 0, 1).reshape(4, 4)),
        'head_norm': f(inp['l0_ml_head_norm']), 'q_norm1': f(inp['l1_q_norm']), 'k_norm1': f(inp['l1_k_norm']),
        'final_norm': f(inp['final_norm']), 'ropeM': rt['ropeM'], 'ropeG': rt['ropeG'],
    }
    for l in (0, 1):
        shared[f'ada_w{l}'] = f(inp[f'l{l}_ada_w']); shared[f'ada_b{l}'] = f(inp[f'l{l}_ada_b'])
        shared[f'norm1_{l}'] = f(inp[f'l{l}_norm1']); shared[f'norm2_{l}'] = f(inp[f'l{l}_norm2'])
        shared[f'w_out{l}'] = f(inp[f'l{l}_w_out']); shared[f'w1_{l}'] = f(inp[f'l{l}_w1']); shared[f'w2_{l}'] = f(inp[f'l{l}_w2'])
    x = f(inp['x']); ctx = f(inp['ctx']); c = f(inp['c']); c_ctx = f(inp['c_ctx'])
    maps = []
    for cidx in range(NCORES):
        b, par = cidx // 2, cidx % 2
        m = dict(shared)
        m['sel'] = np.ascontiguousarray(np.tile(np.eye(2, dtype=np.float32)[par][None, :], (128, 1)))
        m['rows'] = np.concatenate([ctx[b], x[b]], axis=0)
        cc = np.stack([c[b], c_ctx], axis=-1)
        m['cT'] = np.ascontiguousarray(cc.reshape(8, 128, 2).transpose(1, 0, 2))
        maps.append(m)
    return maps


def kernel(**inputs):
    nc = build()
    maps = make_in_maps(inputs)
    res = run_bass_kernel_spmd(nc, maps, core_ids=list(range(NCORES)))
    out = np.empty((NCORES // 2, T, D), np.float32)
    for cidx, r in enumerate(res.results):
        out[cidx // 2, (cidx % 2) * TH:(cidx % 2 + 1) * TH] = r['y']
    return out
```
